# Optimizing a Trainium2 kernel written in Bass

```python
import jax, jax.numpy as jnp
from jax import lax
import numpy as np

D_MODEL = 2048
BATCH = 4
SEQ = 2048
DEPTH = 2

HEAD_DIM = 64
N_Q_HEADS = 16
N_KV_HEADS = 4
Q_PER_KV = N_Q_HEADS // N_KV_HEADS
ATTN_WIDTH = N_Q_HEADS * HEAD_DIM
KV_WIDTH = N_KV_HEADS * HEAD_DIM
WINDOW = 128
BLOCK = 128
ROPE_THETA = 10000.0
SGU_GROUPS = 8
SGU_GROUP_DIM = 128
SGU_WIDTH = SGU_GROUPS * SGU_GROUP_DIM
CHUNK = 128
N_BRANCHES = 2
D_FF = ((-(-8 * D_MODEL // 3) + 255) // 256) * 256
IN_WIDTH = ATTN_WIDTH + 2 * KV_WIDTH + 2 * SGU_WIDTH + N_BRANCHES * D_MODEL
EPS = 1e-6

kernel_name = "hybrid_gated_swa_sgu_block"


def rms_norm(x, g):
    xf = x.astype(jnp.float32)
    y = xf * lax.rsqrt(jnp.mean(xf * xf, axis=-1, keepdims=True) + EPS)
    return (y * g.astype(jnp.float32)).astype(x.dtype)


def rope_tables(seq):
    pos = jnp.arange(seq, dtype=jnp.float32)
    inv_freq = jnp.power(ROPE_THETA, -jnp.arange(0, HEAD_DIM, 2, dtype=jnp.float32) / HEAD_DIM)
    ang = pos[:, None] * inv_freq[None, :]
    return jnp.cos(ang), jnp.sin(ang)


def apply_rope(x, cos, sin):
    xf = x.astype(jnp.float32)
    half = HEAD_DIM // 2
    x1, x2 = xf[..., :half], xf[..., half:]
    c, s = cos[None, :, None, :], sin[None, :, None, :]
    return jnp.concatenate([x1 * c - x2 * s, x2 * c + x1 * s], axis=-1).astype(x.dtype)


def sliding_window_attention(q, k, v, sinks):
    B, S = q.shape[0], q.shape[1]
    nb = S // BLOCK
    qb = q.reshape(B, nb, BLOCK, N_KV_HEADS, Q_PER_KV, HEAD_DIM)
    kb = k.reshape(B, nb, BLOCK, N_KV_HEADS, HEAD_DIM)
    vb = v.reshape(B, nb, BLOCK, N_KV_HEADS, HEAD_DIM)

    def with_prev(t):
        prev = jnp.pad(t[:, :-1], ((0, 0), (1, 0), (0, 0), (0, 0), (0, 0)))
        return jnp.concatenate([prev, t], axis=2)

    kw, vw = with_prev(kb), with_prev(vb)
    scale = HEAD_DIM ** -0.5
    scores = jnp.einsum('bnqhgd,bnkhd->bnhgqk', qb, kw).astype(jnp.float32) * scale
    q_pos = jnp.arange(BLOCK)[:, None] + BLOCK
    k_pos = jnp.arange(2 * BLOCK)[None, :]
    diff = q_pos - k_pos
    band = (diff >= 0) & (diff < WINDOW)
    valid = (jnp.arange(nb)[:, None, None] > 0) | (k_pos >= BLOCK)[None]
    mask = (band[None] & valid)[None, :, None, None]
    scores = jnp.where(mask, scores, -1e30)
    sink = jnp.broadcast_to(
        sinks.astype(jnp.float32).reshape(N_KV_HEADS, Q_PER_KV)[None, None, :, :, None, None],
        scores.shape[:-1] + (1,))
    probs = jax.nn.softmax(jnp.concatenate([scores, sink], axis=-1), axis=-1)[..., :-1]
    out = jnp.einsum('bnhgqk,bnkhd->bnqhgd', probs.astype(v.dtype), vw)
    return out.reshape(B, S, ATTN_WIDTH)


def chunked_sgu(uv, w_s, b_s, ln_g, ln_b):
    B, S = uv.shape[0], uv.shape[1]
    nc = S // CHUNK
    u, v = uv[..., :SGU_WIDTH], uv[..., SGU_WIDTH:]
    vf = v.astype(jnp.float32).reshape(B, S, SGU_GROUPS, SGU_GROUP_DIM)
    mu = jnp.mean(vf, axis=-1, keepdims=True)
    var = jnp.mean(jnp.square(vf - mu), axis=-1, keepdims=True)
    vn = ((vf - mu) * lax.rsqrt(var + EPS) * ln_g.reshape(SGU_GROUPS, SGU_GROUP_DIM)
          + ln_b.reshape(SGU_GROUPS, SGU_GROUP_DIM)).astype(v.dtype)
    vc = vn.reshape(B, nc, CHUNK, SGU_GROUPS, SGU_GROUP_DIM)
    tri = jnp.tril(jnp.ones((CHUNK, CHUNK), dtype=bool))
    w = jnp.where(tri[None], w_s, jnp.zeros_like(w_s))
    s = jnp.einsum('gij,bnjgd->bnigd', w, vc) + jnp.transpose(b_s)[None, None, :, :, None]
    return u * s.reshape(B, S, SGU_WIDTH)


def setup_inputs(seed: int = 0) -> dict:
    key = jax.random.key(seed)
    ks = jax.random.split(key, 17)
    D = D_MODEL

    def nrm(k, shape, scale):
        return jax.random.normal(k, shape, jnp.float32) * scale

    return {
        "x": nrm(ks[0], (BATCH, SEQ, D), 1.0),
        "mix_norm": 1.0 + nrm(ks[1], (DEPTH, D), 0.02),
        "w_in": nrm(ks[2], (DEPTH, D, IN_WIDTH), D ** -0.5),
        "q_norm": 1.0 + nrm(ks[3], (DEPTH, HEAD_DIM), 0.02),
        "k_norm": 1.0 + nrm(ks[4], (DEPTH, HEAD_DIM), 0.02),
        "sinks": nrm(ks[5], (DEPTH, N_Q_HEADS), 0.5),
        "sgu_ln_g": 1.0 + nrm(ks[6], (DEPTH, SGU_WIDTH), 0.02),
        "sgu_ln_b": nrm(ks[7], (DEPTH, SGU_WIDTH), 0.02),
        "w_spatial": nrm(ks[8], (DEPTH, SGU_GROUPS, CHUNK, CHUNK), 0.5 * CHUNK ** -0.5),
        "b_spatial": 1.0 + nrm(ks[9], (DEPTH, SGU_GROUPS, CHUNK), 0.02),
        "w_attn_branch": nrm(ks[10], (DEPTH, ATTN_WIDTH, D), ATTN_WIDTH ** -0.5),
        "w_sgu_branch": nrm(ks[11], (DEPTH, SGU_WIDTH, D), SGU_WIDTH ** -0.5),
        "w_out": nrm(ks[12], (DEPTH, D, D), D ** -0.5),
        "ffn_norm": 1.0 + nrm(ks[13], (DEPTH, D), 0.02),
        "w_gate": nrm(ks[14], (DEPTH, D, D_FF), D ** -0.5),
        "w_up": nrm(ks[15], (DEPTH, D, D_FF), D ** -0.5),
        "w_down": nrm(ks[16], (DEPTH, D_FF, D), D_FF ** -0.5),
    }


def reference(x, mix_norm, w_in, q_norm, k_norm, sinks, sgu_ln_g, sgu_ln_b, w_spatial,
              b_spatial, w_attn_branch, w_sgu_branch, w_out, ffn_norm, w_gate, w_up, w_down):
    B, S = x.shape[0], x.shape[1]
    cos, sin = rope_tables(S)
    cuts = [ATTN_WIDTH, ATTN_WIDTH + KV_WIDTH, ATTN_WIDTH + 2 * KV_WIDTH,
            ATTN_WIDTH + 2 * KV_WIDTH + 2 * SGU_WIDTH]
    for l in range(DEPTH):
        h = rms_norm(x, mix_norm[l])
        proj = h @ w_in[l]
        q, k, v, uv, gate_logits = jnp.split(proj, cuts, axis=-1)
        q = apply_rope(rms_norm(q.reshape(B, S, N_Q_HEADS, HEAD_DIM), q_norm[l]), cos, sin)
        k = apply_rope(rms_norm(k.reshape(B, S, N_KV_HEADS, HEAD_DIM), k_norm[l]), cos, sin)
        v = v.reshape(B, S, N_KV_HEADS, HEAD_DIM)
        branch_a = sliding_window_attention(q, k, v, sinks[l]) @ w_attn_branch[l]
        branch_b = chunked_sgu(jax.nn.gelu(uv), w_spatial[l], b_spatial[l],
                               sgu_ln_g[l], sgu_ln_b[l]) @ w_sgu_branch[l]
        gates = jax.nn.sigmoid(gate_logits)
        merged = gates[..., :D_MODEL] * branch_a + gates[..., D_MODEL:] * branch_b
        x = x + merged @ w_out[l]
        h2 = rms_norm(x, ffn_norm[l])
        x = x + (jax.nn.silu(h2 @ w_gate[l]) * (h2 @ w_up[l])) @ w_down[l]
    return x
```

```python
import numpy as np
import concourse.bass as bass
import concourse.mybir as mybir
from concourse.bass_utils import run_bass_kernel_spmd

AF = mybir.ActivationFunctionType
ALU = mybir.AluOpType
AX = mybir.AxisListType
F32 = mybir.dt.float32
BF16 = mybir.dt.bfloat16

ENGS = ("pe", "act", "dve", "pool", "sp")
EPS = 1e-6
D = 2048
NCH = 16
DFF = 5632
KOFF, VOFF, UOFF, VSOFF, GAOFF, GBOFF = 1024, 1280, 1536, 2560, 3584, 5632
NEG = -30000.0
TW = 640
BCW = 144
FFN_PARTS = [(0, 12), (12, 12), (24, 12), (36, 8)]


class Buf:
    __slots__ = ("name", "w", "r", "excl")

    def __init__(self, name, excl=False):
        self.name = name
        self.w = None
        self.r = {}
        self.excl = excl


class Sched:
    def __init__(self, nc, same_eng_sync=True):
        self.nc = nc
        self.q = {e: [] for e in ENGS}
        self.semh = {}
        self.cnt = {}
        for e in ENGS:
            self.semh[e] = nc.alloc_semaphore("s_" + e)
            self.cnt[e] = 0
        self.seen = {e: {} for e in ENGS}
        self.same_eng_sync = same_eng_sync
        self.pe_pending = False
        self.nwaits = 0

    def _need(self, eng, ev, needs):
        key, val = ev
        if key == eng and (eng == "pe" or not self.same_eng_sync):
            return
        if self.seen[eng].get(key, 0) >= val:
            return
        if needs.get(key, 0) < val:
            needs[key] = val

    def _deps(self, eng, reads, writes):
        needs = {}
        for b in reads:
            if b.w is not None:
                self._need(eng, b.w, needs)
            if b.excl:
                for k, v in b.r.items():
                    if k != eng:
                        self._need(eng, (k, v), needs)
        for b in writes:
            if b.w is not None:
                self._need(eng, b.w, needs)
            for k, v in b.r.items():
                if k == eng:
                    continue
                self._need(eng, (k, v), needs)
        for key, val in needs.items():
            self.seen[eng][key] = val
            h = self.semh[key]
            self.nwaits += 1
            self.q[eng].append(lambda e, h=h, val=val: e.wait_ge(h, val))

    def _mark(self, ev, reads, writes):
        k, v = ev
        for b in reads:
            if b.r.get(k, 0) < v:
                b.r[k] = v
        for b in writes:
            b.w = ev
            b.r = {}

    def op(self, eng, meth, reads=(), writes=(), inc=True, **kw):
        self._deps(eng, reads, writes)
        if inc:
            self.cnt[eng] += 1
            ev = (eng, self.cnt[eng])
            h = self.semh[eng]
            self.q[eng].append(lambda e, meth=meth, kw=kw, h=h: getattr(e, meth)(**kw).then_inc(h, 1))
            if eng == "pe":
                self.pe_pending = False
        else:
            assert eng == "pe"
            ev = (eng, self.cnt[eng] + 1)
            self.q[eng].append(lambda e, meth=meth, kw=kw: getattr(e, meth)(**kw))
            self.pe_pending = True
        self._mark(ev, reads, writes)
        return ev

    def dma(self, eng, key, out, in_, reads=(), writes=()):
        if key is None:
            key = "b_" + (writes[0].name if len(writes) else reads[0].name)
        if key not in self.semh:
            self.semh[key] = self.nc.alloc_semaphore("d_" + key)
            self.cnt[key] = 0
        self._deps(eng, reads, writes)
        self.cnt[key] += 16
        ev = (key, self.cnt[key])
        h = self.semh[key]
        self.q[eng].append(lambda e, out=out, in_=in_, h=h: e.dma_start(out=out, in_=in_).then_inc(h, 16))
        self._mark(ev, reads, writes)
        return ev

    def wait_event(self, eng, ev):
        key, val = ev
        if self.seen[eng].get(key, 0) < val:
            self.seen[eng][key] = val
            h = self.semh[key]
            self.q[eng].append(lambda e, h=h, val=val: e.wait_ge(h, val))

    def emit(self):
        assert not self.pe_pending
        q = self.q
        with self.nc.Block() as block:
            @block.tensor
            def _(e):
                for f in q["pe"]:
                    f(e)

            @block.scalar
            def _(e):
                for f in q["act"]:
                    f(e)

            @block.vector
            def _(e):
                for f in q["dve"]:
                    f(e)

            @block.gpsimd
            def _(e):
                for f in q["pool"]:
                    f(e)

            @block.sync
            def _(e):
                for f in q["sp"]:
                    f(e)


class Ring:
    def __init__(self, items):
        self.items = items
        self.i = 0

    def next(self):
        it = self.items[self.i % len(self.items)]
        self.i += 1
        return it


def build(stop_after=None, dumps=()):
    nc = bass.Bass("TRN2", target_bir_lowering=False)
    S = Sched(nc)
    dumps = set(dumps)

    x_in = nc.dram_tensor("x", [1280, D], F32, kind="ExternalInput").ap()
    cs_in = nc.dram_tensor("cs", [1280, 64], F32, kind="ExternalInput").ap()
    cf_in = nc.dram_tensor("cf", [128, 256], F32, kind="ExternalInput").ap()
    cm_in = nc.dram_tensor("cm", [128, 1536], F32, kind="ExternalInput").ap()
    nrm_in = nc.dram_tensor("nrm", [64, 128], F32, kind="ExternalInput").ap()
    bc_in = nc.dram_tensor("bc", [2, BCW], F32, kind="ExternalInput").ap()
    lngb_in = nc.dram_tensor("lngb", [2, 2, 1024], F32, kind="ExternalInput").ap()
    bsp_in = nc.dram_tensor("bsp", [2, 1024], F32, kind="ExternalInput").ap()
    wsp_in = nc.dram_tensor("w_spatial", [2, 8, 128, 128], F32, kind="ExternalInput").ap()
    w_in = nc.dram_tensor("w_in", [2, D, 7680], F32, kind="ExternalInput").ap()
    w_attn = nc.dram_tensor("w_attn", [2, 1024, D], F32, kind="ExternalInput").ap()
    w_sgu = nc.dram_tensor("w_sgu", [2, 1024, D], F32, kind="ExternalInput").ap()
    w_out = nc.dram_tensor("w_out", [2, D, D], F32, kind="ExternalInput").ap()
    w_gate = nc.dram_tensor("w_gate", [2, D, DFF], F32, kind="ExternalInput").ap()
    w_up = nc.dram_tensor("w_up", [2, D, DFF], F32, kind="ExternalInput").ap()
    w_down = nc.dram_tensor("w_down", [2, DFF, D], F32, kind="ExternalInput").ap()
    out_d = nc.dram_tensor("out", [1024, D], F32, kind="ExternalOutput").ap()

    def sb(name, shape, dt):
        return nc.alloc_sbuf_tensor("sb_" + name, shape, dt)

    xT = sb("xT", [128, NCH, TW], F32)
    cb = sb("cb", [128, 48, TW], BF16)
    h0T = sb("h0T", [128, NCH, 128], BF16)
    NSL = 4
    wsl = [sb(f"wsl{i}", [128, 4096], BF16) for i in range(NSL)]
    kT = sb("kT", [128, 7, 4, 128], BF16)
    Vx = sb("Vx", [128, 7, 4, 65], BF16)
    xin = [sb(f"xin{i}", [128, 1024], F32) for i in range(2)]
    cf = sb("cf", [128, 256], F32)
    cmb = sb("cmb", [128, 1536], BF16)
    ident_bf = sb("ident_bf", [128, 128], BF16)
    ones_bf = sb("ones_bf", [128, 128], BF16)
    epst = sb("epst", [128, 1], F32)
    nrm_sb = sb("nrm_sb", [64, 128], F32)
    gT = sb("gT", [128, 64], F32)
    cs_sb = sb("cs_sb", [128, 10, 64], F32)
    bcl = sb("bcl", [128, BCW], F32)
    esink = sb("esink", [128, 16], F32)
    lngb = sb("lngb", [128, 2, 512], F32)
    WsT = sb("WsT", [128, 8, 128], BF16)
    bsp_hi = sb("bsp_hi", [1, 1024], BF16)
    bsp_lo = sb("bsp_lo", [1, 1024], BF16)
    sqb = [sb(f"sq{i}", [128, 512], BF16) for i in range(3)]
    rstd = sb("rstd", [128, 512], F32)
    NTM = 5
    tm = [sb(f"tm{i}", [128, 512], F32) for i in range(NTM)]
    sm = [sb(f"sm{i}", [128, 16], F32) for i in range(8)]
    qrot = [sb(f"qrot{i}", [128, 512], BF16) for i in range(2)]
    qTb = [sb(f"qT{i}", [128, 8, 128], BF16) for i in range(2)]
    Pb = [sb(f"P{i}", [128, 512], BF16) for i in range(8)]
    atm = [sb(f"atm{i}", [128, 512], BF16) for i in range(2)]
    vnb = [sb(f"vn{i}", [128, 512], BF16) for i in range(2)]
    psum = [nc.alloc_psum_tensor(f"ps{i}", [128, 512], F32) for i in range(8)]

    ident = cf[:, 0:128]
    tri = cf[:, 128:256]
    maskcur = cmb[:, 0:512]
    maskprev = cmb[:, 512:1024]
    maskfirst = cmb[:, 1024:1536]

    B = Buf
    b_xT = {(c, k): B(f"xT{c}_{k}") for c in range(NCH) for k in range(5)}
    b_cb = {(i, k): B(f"cb{i}_{k}") for i in range(48) for k in range(5)}
    b_h0T = B("h0T")
    b_wsl = [B(f"wsl{i}") for i in range(NSL)]
    b_wslB = [B(f"wslB{i}") for i in range(NSL)]

    def wb(sl):
        return [b_wsl[sl], b_wslB[sl]]
    b_kT = [B(f"kT{i}") for i in range(7)]
    b_Vx = [B(f"Vx{i}") for i in range(7)]
    b_xin = [B("xin0"), B("xin1")]
    b_cf, b_cmb, b_identbf, b_ones, b_eps, b_nrm, b_gT, b_cs = (B("cf"), B("cmb"), B("identbf"), B("ones"),
                                                                 B("eps"), B("nrm"), B("gT"), B("cs"))
    b_bcl, b_esink, b_lngb, b_WsT, b_bsp = B("bcl"), B("esink"), B("lngb"), B("WsT"), B("bsp")
    b_sq = [B(f"sq{i}") for i in range(3)]
    b_rstd = B("rstd")
    b_tm = [B(f"tm{i}") for i in range(NTM)]
    b_sm = [B(f"sm{i}") for i in range(8)]
    b_qrot = [B("qrot0"), B("qrot1")]
    b_qT = [B("qT0"), B("qT1")]
    b_P = [B(f"P{i}") for i in range(8)]
    b_atm = [B("atm0"), B("atm1")]
    b_vn = [B("vn0"), B("vn1")]
    b_ps = [B(f"ps{i}", excl=True) for i in range(8)]

    psr = Ring(list(range(8)))
    psA = Ring([0, 1])
    psB = Ring([2, 3, 4, 5, 6, 7])
    sqr = Ring(list(range(3)))
    tmr = Ring(list(range(NTM)))
    smr = Ring(list(range(8)))
    qrr = Ring([0, 1])
    qtr = Ring([0, 1])
    Pr = Ring(list(range(8)))
    atr = Ring([0, 1])
    vnr = Ring([0, 1])
    xir = Ring([0, 1])
    evr = Ring(["act", "dve"])

    out_events = []
    dump_tensors = {}

    def dump(name, ap, bufs, shape, dt):
        if name not in dumps:
            return
        t = nc.dram_tensor("dbg_" + name, list(shape), dt, kind="ExternalOutput").ap()
        out_events.append(S.dma("sp", "dbg_" + name, t, ap, reads=bufs))
        dump_tensors[name] = True

    HT, AT, GT, MG, ACT0 = 0, 16, 24, 32, 16

    def cbv(i, off, n):
        return cb[:, i, off:off + n]

    def blocks_of(off, n):
        return list(range(off // 128, (off + n + 127) // 128))

    def cbb(i, off, n):
        return [b_cb[(i, k)] for k in blocks_of(off, n)]

    def xTb(c, off, n):
        return [b_xT[(c, k)] for k in blocks_of(off, n)]

    class WStream:
        def __init__(self):
            self.plan = []
            self.next_issue = 0
            self.next_get = 0

        def add(self, pieces):
            self.plan.append(pieces)

        def _issue(self, i):
            slot = i % NSL
            for (src, off, nch, ncols) in self.plan[i]:
                dst = wsl[slot][:, off:off + nch * ncols].rearrange("p (c n) -> p c n", c=nch)
                if nch * ncols > 2048:
                    wr = [b_wsl[slot], b_wslB[slot]]
                else:
                    wr = [b_wsl[slot]][:] if off == 0 else [b_wslB[slot]]
                S.dma("pool", None, dst, src, writes=wr)

        def start(self):
            while self.next_issue < min(NSL, len(self.plan)):
                self._issue(self.next_issue)
                self.next_issue += 1

        def get(self):
            i = self.next_get
            self.next_get += 1
            assert i < self.next_issue, "slab not issued (ring too small for access pattern)"
            return i % NSL

        def done(self, n=1):
            for _ in range(n):
                if self.next_issue < len(self.plan):
                    self._issue(self.next_issue)
                    self.next_issue += 1

    W = WStream()

    def wsrc(w, l, r0, nch, c0, ncols):
        return (w[l, r0:r0 + nch * 128, c0:c0 + ncols].rearrange("(c p) n -> p c n", p=128), nch, ncols)

    def plan_layer(l):
        def one(w, r0, nch, c0, ncols, off=0):
            src, nch_, ncols_ = wsrc(w, l, r0, nch, c0, ncols)
            return (src, off, nch_, ncols_)
        for c0 in (KOFF, VOFF):
            W.add([one(w_in, 0, 16, c0, 256)])
        for j in range(4):
            W.add([one(w_in, 0, 16, j * 256, 256)])
        for j in range(4):
            W.add([one(w_in, 0, 16, UOFF + j * 256, 256)])
        for j in range(4):
            W.add([one(w_in, 0, 16, VSOFF + j * 256, 256)])
        for s in range(8):
            W.add([one(w_in, 0, 16, GAOFF + s * 256, 256)])
            W.add([one(w_attn, 0, 8, s * 256, 256, 0), one(w_sgu, 0, 8, s * 256, 256, 2048)])
            W.add([one(w_in, 0, 16, GBOFF + s * 256, 256)])
        for s in range(8):
            W.add([one(w_out, 0, 16, s * 256, 256)])
        for (fc0, nfc) in FFN_PARTS:
            for j in range(nfc // 2):
                W.add([one(w_gate, 0, 16, (fc0 + 2 * j) * 128, 256)])
                W.add([one(w_up, 0, 16, (fc0 + 2 * j) * 128, 256)])
            for s in range(8):
                W.add([one(w_down, fc0 * 128, nfc, s * 256, 256)])

    def wv(slot, off, nch, ncols):
        return wsl[slot][:, off:off + nch * ncols].rearrange("p (c n) -> p c n", c=nch)

    def evac_copy(eng, out, in_, reads, writes):
        if eng == "act":
            S.op("act", "activation", reads=reads, writes=writes, out=out, in_=in_, func=AF.Copy)
        else:
            S.op(eng, "tensor_copy", reads=reads, writes=writes, out=out, in_=in_)

    def rms_stats(src_fn, n, src_bufs_fn):
        p = psr.next()
        for c in range(NCH):
            s = sqr.next()
            S.op("act", "activation", reads=src_bufs_fn(c), writes=[b_sq[s]],
                 out=sqb[s][:, 0:n], in_=src_fn(c), func=AF.Square)
            S.op("pe", "matmul", reads=[b_sq[s], b_ones], writes=[b_ps[p]], inc=True,
                 out=psum[p][:, 0:n], lhsT=ones_bf[:], rhs=sqb[s][:, 0:n], start=(c == 0), stop=(c == NCH - 1))
        S.op("act", "activation", reads=[b_ps[p], b_eps], writes=[b_rstd],
             out=rstd[:, 0:n], in_=psum[p][:, 0:n], func=AF.Ln, bias=epst[:, 0:1], scale=1.0 / D)
        S.op("act", "activation", reads=[b_rstd], writes=[b_rstd], out=rstd[:, 0:n], in_=rstd[:, 0:n], func=AF.Exp, scale=-0.5)

    def stats_add(c, off, n, bank, first, last):
        s_ = sqr.next()
        S.op("act", "activation", reads=xTb(c, off, n), writes=[b_sq[s_]], out=sqb[s_][:, 0:n], in_=xT[:, c, off:off + n],
             func=AF.Square)
        S.op("pe", "matmul", reads=[b_sq[s_], b_ones], writes=[b_ps[bank]], inc=True,
             out=psum[bank][:, 0:n], lhsT=ones_bf[:], rhs=sqb[s_][:, 0:n], start=first, stop=last)

    class StatsQ:
        def __init__(self, lag=2):
            self.q = []
            self.lag = lag

        def push(self, *a):
            self.q.append(a)
            while len(self.q) > self.lag:
                stats_add(*self.q.pop(0))

        def flush(self):
            while self.q:
                stats_add(*self.q.pop(0))

    def stats_finish(bank, n):
        S.op("act", "activation", reads=[b_ps[bank], b_eps], writes=[b_rstd],
             out=rstd[:, 0:n], in_=psum[bank][:, 0:n], func=AF.Ln, bias=epst[:, 0:1], scale=1.0 / D)
        S.op("act", "activation", reads=[b_rstd], writes=[b_rstd], out=rstd[:, 0:n], in_=rstd[:, 0:n], func=AF.Exp, scale=-0.5)

    def rmsnorm_cols(gcol0, tiles, pre=None):
        for ti, (off, n) in enumerate(tiles):
            if pre is None:
                rms_stats(lambda c: xT[:, c, off:off + n], n, lambda c: xTb(c, off, n))
            else:
                stats_finish(pre[ti], n)
            for c in range(NCH):
                S.op("dve", "scalar_tensor_tensor", reads=xTb(c, off, n) + [b_rstd, b_gT], writes=cbb(HT + c, off, n),
                     out=cbv(HT + c, off, n), in0=xT[:, c, off:off + n], scalar=gT[:, gcol0 + c:gcol0 + c + 1],
                     in1=rstd[:, 0:n], op0=ALU.mult, op1=ALU.mult)

    def head_norm_rope(p, col0, nh, gcol, blk, dst, dst_buf):
        w = nh * 64
        src = psum[p][:, col0:col0 + w]
        t_sq, t_n = tmr.next(), tmr.next()
        s_ss = smr.next()
        S.op("act", "activation", reads=[b_ps[p]], writes=[b_tm[t_sq]], out=tm[t_sq][:, 0:w], in_=src, func=AF.Square)
        S.op("dve", "tensor_reduce", reads=[b_tm[t_sq]], writes=[b_sm[s_ss]], out=sm[s_ss][:, 0:nh],
             in_=tm[t_sq][:, 0:w].rearrange("p (h d) -> p h d", h=nh), axis=AX.X, op=ALU.add)
        S.op("act", "activation", reads=[b_sm[s_ss], b_eps], writes=[b_sm[s_ss]], out=sm[s_ss][:, 0:nh],
             in_=sm[s_ss][:, 0:nh], func=AF.Ln, bias=epst[:, 0:1], scale=1.0 / 64)
        S.op("act", "activation", reads=[b_sm[s_ss]], writes=[b_sm[s_ss]], out=sm[s_ss][:, 0:nh], in_=sm[s_ss][:, 0:nh],
             func=AF.Exp, scale=-0.5)
        n3 = tm[t_n][:, 0:w].rearrange("p (h d) -> p h d", h=nh)
        S.op("dve", "tensor_tensor", reads=[b_ps[p], b_sm[s_ss]], writes=[b_tm[t_n]], out=n3,
             in0=src.rearrange("p (h d) -> p h d", h=nh), in1=sm[s_ss][:, 0:nh].unsqueeze(2).to_broadcast([128, nh, 64]),
             op=ALU.mult)
        S.op("dve", "tensor_tensor", reads=[b_tm[t_n], b_bcl], writes=[b_tm[t_n]], out=n3, in0=n3,
             in1=bcl[:, gcol:gcol + 64].unsqueeze(1).to_broadcast([128, nh, 64]), op=ALU.mult)
        n4 = tm[t_n][:, 0:w].rearrange("p (h t d) -> p h t d", h=nh, t=2)
        x1, x2 = n4[:, :, 0, :], n4[:, :, 1, :]
        cosb = cs_sb[:, blk, 0:32].unsqueeze(1).to_broadcast([128, nh, 32])
        sinb = cs_sb[:, blk, 32:64].unsqueeze(1).to_broadcast([128, nh, 32])
        ta = tm[t_sq][:, 0:w].rearrange("p (h t d) -> p h t d", h=nh, t=2)
        d4 = dst.rearrange("p (h t d) -> p h t d", h=nh, t=2)
        rb = [b_tm[t_n], b_cs]
        S.op("dve", "tensor_tensor", reads=rb, writes=[b_tm[t_sq]], out=ta[:, :, 0, :], in0=x1, in1=cosb, op=ALU.mult)
        S.op("dve", "tensor_tensor", reads=rb, writes=[b_tm[t_sq]], out=ta[:, :, 1, :], in0=x2, in1=sinb, op=ALU.mult)
        S.op("dve", "tensor_tensor", reads=[b_tm[t_sq]], writes=[dst_buf], out=d4[:, :, 0, :], in0=ta[:, :, 0, :],
             in1=ta[:, :, 1, :], op=ALU.subtract)
        S.op("dve", "tensor_tensor", reads=rb + [dst_buf], writes=[b_tm[t_sq]], out=ta[:, :, 0, :], in0=x2, in1=cosb, op=ALU.mult)
        S.op("dve", "tensor_tensor", reads=rb, writes=[b_tm[t_sq]], out=ta[:, :, 1, :], in0=x1, in1=sinb, op=ALU.mult)
        S.op("dve", "tensor_tensor", reads=[b_tm[t_sq]], writes=[dst_buf], out=d4[:, :, 1, :], in0=ta[:, :, 0, :],
             in1=ta[:, :, 1, :], op=ALU.add)

    def gelu_tanh(eng_pool, src, src_bufs, dst, dst_bufs, n):
        S.op("act", "activation", reads=src_bufs, writes=dst_bufs, out=dst, in_=src, func=AF.Gelu_apprx_tanh)

    S.dma("sp", None, cf[:], cf_in[:, :], writes=[b_cf])
    S.dma("pool", None, cmb[:], cm_in[:, :], writes=[b_cmb])
    S.dma("sp", None, nrm_sb[:], nrm_in[:, :], writes=[b_nrm])
    S.dma("sp", None, cs_sb[:], cs_in.rearrange("(b p) d -> p b d", p=128), writes=[b_cs])
    S.op("dve", "memset", writes=[b_ones], ap=ones_bf[:], constant=1.0)
    S.op("dve", "memset", writes=[b_eps], ap=epst[:], constant=EPS)
    S.op("dve", "memset", writes=b_Vx, ap=Vx[:], constant=1.0)
    S.op("dve", "memset", writes=b_kT, ap=kT[64:128], constant=0.0)
    for i_ in range(2):
        S.op("dve", "memset", writes=[b_qT[i_]], ap=qTb[i_][64:128], constant=0.0)
    S.op("dve", "tensor_copy", reads=[b_cf], writes=[b_identbf], out=ident_bf[:], in_=ident)
    p = psr.next()
    S.op("pe", "transpose", reads=[b_nrm, b_cf], writes=[b_ps[p]], out=psum[p][:, 0:64], in_=nrm_sb[:], identity=cf[0:64, 0:64])
    S.op("dve", "tensor_copy", reads=[b_ps[p]], writes=[b_gT], out=gT[:], in_=psum[p][:, 0:64])

    for l in range(2):
        plan_layer(l)
    plan_one = len(W.plan) // 2
    W.plan = W.plan + W.plan
    W.start()

    def run_pipeline(gens):
        active = []
        it = iter(gens)
        while True:
            nxt = next(it, None)
            keep = []
            for g in reversed(active):
                try:
                    next(g)
                    keep.append(g)
                except StopIteration:
                    pass
            keep.reverse()
            if nxt is not None:
                try:
                    next(nxt)
                    keep.append(nxt)
                except StopIteration:
                    pass
            active = keep
            if nxt is None and not active:
                break

    def load_x_blocks(blocks, colof):
        for b in blocks:
            for hf in range(2):
                xi = xir.next()
                S.dma("sp", None, xin[xi][:], x_in[b * 128:(b + 1) * 128, hf * 1024:(hf + 1) * 1024], writes=[b_xin[xi]])
                for g in range(2):
                    p = psr.next()
                    for j in range(4):
                        cc = g * 4 + j
                        S.op("pe", "transpose", reads=[b_xin[xi], b_cf], writes=[b_ps[p]], inc=(j == 3),
                             out=psum[p][:, j * 128:(j + 1) * 128], in_=xin[xi][:, cc * 128:(cc + 1) * 128], identity=ident)
                    c0 = hf * 8 + g * 4
                    if b == 0:
                        evac_copy(evr.next(), h0T[:, c0:c0 + 4, :], psum[p][:].rearrange("p (j n) -> p j n", j=4),
                                  [b_ps[p]], [b_h0T])
                    else:
                        k = colof[b] // 128
                        evac_copy(evr.next(), xT[:, c0:c0 + 4, colof[b]:colof[b] + 128],
                                  psum[p][:].rearrange("p (j n) -> p j n", j=4), [b_ps[p]],
                                  [b_xT[(c0 + j, k)] for j in range(4)])

    carry = {}

    def layer(wave, l):
        first = (wave == 0)
        if first and l == 0:
            F = [1, 2, 3, 4, 5]
            colof = {b: (b - 1) * 128 for b in F}
            kvb = 0
            slot = {0: 0, 1: 1, 2: 2, 3: 3, 4: 4, 5: 6}
            tt = [(0, 320), (320, 320)]
            ntiles = tt
        elif first:
            F = [2, 3, 4, 5]
            colof = {b: (b - 1) * 128 for b in [1, 2, 3, 4, 5]}
            kvb = 1
            slot = {1: 0, 2: 1, 3: 2, 4: 3, 5: 5}
            tt = [(128, 512)]
            ntiles = [(0, 320), (320, 320)]
        else:
            F = [6, 7, 8, 9]
            colof = {b: (b - 6) * 128 for b in F}
            kvb = None
            slot = {6: 0, 7: 1, 8: 2, 9: 3, 5: (6 if l == 0 else 5)}
            tt = [(0, 512)]
            ntiles = tt
        gmix, gffn = l * 32, l * 32 + 16
        pre_mix = carry.pop("pre", None)
        if pre_mix is not None:
            rmsnorm_cols(gmix, ntiles, pre=pre_mix)

        S.dma("sp", None, bcl[:], bc_in[l:l + 1, :].partition_broadcast(128), writes=[b_bcl])
        S.op("act", "activation", reads=[b_bcl], writes=[b_esink], out=esink[:], in_=bcl[:, 128:144], func=AF.Exp)
        for hh in range(2):
            tf, tt_ = tmr.next(), tmr.next()
            bf_, bt_ = tm[tf][0:1, 0:512], tm[tt_][0:1, 0:512]
            hsl = slice(hh * 512, (hh + 1) * 512)
            S.dma("sp", None, bf_, bsp_in[l:l + 1, hsl], writes=[b_tm[tf]])
            S.op("dve", "tensor_copy", reads=[b_tm[tf]], writes=[b_bsp], out=bsp_hi[0:1, hsl], in_=bf_)
            S.op("dve", "tensor_copy", reads=[b_bsp], writes=[b_tm[tt_]], out=bt_, in_=bsp_hi[0:1, hsl])
            S.op("dve", "tensor_tensor", reads=[b_tm[tf], b_tm[tt_]], writes=[b_tm[tt_]], out=bt_, in0=bf_, in1=bt_, op=ALU.subtract)
            S.op("dve", "tensor_copy", reads=[b_tm[tt_]], writes=[b_bsp], out=bsp_lo[0:1, hsl], in_=bt_)
        for hf in range(2):
            xi = xir.next()
            S.dma("sp", None, xin[xi][:, 0:512].rearrange("p (g j) -> p g j", g=4),
                  wsp_in[l, hf * 4:(hf + 1) * 4].rearrange("g i j -> i g j"), writes=[b_xin[xi]])
            p = psr.next()
            for g in range(4):
                S.op("pe", "transpose", reads=[b_xin[xi], b_cf], writes=[b_ps[p]], inc=(g == 3),
                     out=psum[p][:, g * 128:(g + 1) * 128], in_=xin[xi][:, g * 128:(g + 1) * 128], identity=ident)
            S.op("dve", "tensor_tensor", reads=[b_ps[p], b_cf], writes=[b_WsT],
                 out=WsT[:, hf * 4:(hf + 1) * 4, :], in0=psum[p][:].rearrange("p (g i) -> p g i", g=4),
                 in1=tri.unsqueeze(1).to_broadcast([128, 4, 128]), op=ALU.mult)

        if l == 0:
            load_x_blocks(([0] if first else []) + F, colof)

        if kvb == 0:
            rms_stats(lambda c: h0T[:, c, :], 128, lambda c: [b_h0T])
            for c in range(NCH):
                S.op("dve", "scalar_tensor_tensor", reads=[b_h0T, b_rstd, b_gT], writes=[b_h0T],
                     out=h0T[:, c, :], in0=h0T[:, c, :], scalar=gT[:, gmix + c:gmix + c + 1], in1=rstd[:, 0:128],
                     op0=ALU.mult, op1=ALU.mult)
        if pre_mix is None:
            rmsnorm_cols(gmix, ntiles)
        dump(f"hT_{wave}_{l}", cb[:, HT:HT + 16, :], [b_cb[(HT + c, k)] for c in range(16) for k in range(5)], [128, 16, TW], BF16)
        if stop_after == f"norm_{wave}_{l}":
            return False

        def hsrc(b, c):
            if b == 0:
                return h0T[:, c, :], [b_h0T]
            return cbv(HT + c, colof[b], 128), [b_cb[(HT + c, colof[b] // 128)]]

        def tm_proj(b, slots, p):
            for si, sl in enumerate(slots):
                wvv = wv(sl, 0, 16, 256)
                for c in range(NCH):
                    ha, hb = hsrc(b, c)
                    S.op("pe", "matmul", reads=hb + wb(sl), writes=[b_ps[p]], inc=(c == NCH - 1 and si == 1),
                         out=psum[p][:, si * 256:(si + 1) * 256], lhsT=ha, rhs=wvv[:, c, :], start=(c == 0), stop=(c == NCH - 1))

        sl_k, sl_v = W.get(), W.get()
        kvblocks = ([kvb] if kvb is not None else []) + F

        def kv_item(b, last):
            p = psA.next()
            tm_proj(b, [sl_k, sl_v], p)
            if last:
                W.done(2)
            yield
            s_ = slot[b]
            S.op("act", "activation", reads=[b_ps[p]], writes=[b_Vx[s_]], out=Vx[:, s_, :, 0:64],
                 in_=psum[p][:, 256:512].rearrange("p (h d) -> p h d", h=4), func=AF.Copy)
            qi = qrr.next()
            head_norm_rope(p, 0, 4, 64, b, qrot[qi][:, 0:256], b_qrot[qi])
            yield
            pt = psB.next()
            ptb = psum[pt][:].bitcast(BF16)
            for h in range(4):
                S.op("pe", "transpose", reads=[b_qrot[qi], b_identbf], writes=[b_ps[pt]], inc=(h == 3),
                     out=ptb[0:64, h * 128:(h + 1) * 128], in_=qrot[qi][:, h * 64:(h + 1) * 64], identity=ident_bf[:])
            evac_copy("act", kT[0:64, s_, :, :], ptb[0:64, 0:512].rearrange("p (h n) -> p h n", h=4), [b_ps[pt]], [b_kT[s_]])

        qslabs = {}

        def attn_item(j, b, firstj, lastj):
            if firstj:
                qslabs[j] = (W.get(), W.get())
            sl_a, sl_b = qslabs[j]
            p = psA.next()
            tm_proj(b, [sl_a, sl_b], p)
            if lastj:
                W.done(2)
                if j == 1:
                    ust["q1done"] = True
                    W.done(ust["pend"])
                    ust["pend"] = 0
            yield
            qi = qrr.next()
            head_norm_rope(p, 0, 8, 0, b, qrot[qi][:, 0:512], b_qrot[qi])
            yield
            pt = psB.next()
            ptb = psum[pt][:].bitcast(BF16)
            for h in range(8):
                S.op("pe", "transpose", reads=[b_qrot[qi], b_identbf], writes=[b_ps[pt]], inc=(h == 7),
                     out=ptb[0:64, h * 128:(h + 1) * 128], in_=qrot[qi][:, h * 64:(h + 1) * 64], identity=ident_bf[:])
            qt = qtr.next()
            evac_copy("act", qTb[qt][0:64, :, :], ptb[0:64, :].rearrange("p (h n) -> p h n", h=8), [b_ps[pt]], [b_qT[qt]])
            yield
            sc, sp_ = slot[b], slot[b - 1]
            pis_all = []
            for kk in range(2):
                kvh = 2 * j + kk
                rhs_q = qTb[qt][:, kk * 4:(kk + 1) * 4, :].rearrange("p h n -> p (h n)")
                pis = []
                for (ss, mk) in ((sp_, (maskfirst if (first and b == 2) else maskprev)), (sc, maskcur)):
                    ps_ = psB.next()
                    S.op("pe", "matmul", reads=[b_kT[ss], b_qT[qt]], writes=[b_ps[ps_]], inc=False,
                         out=psum[ps_][:], lhsT=kT[:, ss, kvh, :], rhs=rhs_q, start=True, stop=False)
                    S.op("pe", "matmul", reads=[b_identbf, b_cmb], writes=[b_ps[ps_]], inc=True,
                         out=psum[ps_][:], lhsT=ident_bf[:], rhs=mk, start=False, stop=True)
                    pi = Pr.next()
                    S.op("act", "activation", reads=[b_ps[ps_]], writes=[b_P[pi]], out=Pb[pi][:], in_=psum[ps_][:],
                         func=AF.Exp, scale=0.125)
                    pis.append((pi, ss))
                pis_all.append(pis)
            yield
            ai = atr.next()
            for kk in range(2):
                kvh = 2 * j + kk
                pis = pis_all[kk]
                po = psB.next()
                pov = psum[po][:].rearrange("p (g n) -> p g n", g=4)
                for g in range(4):
                    for idx, (pi, ss) in enumerate(pis):
                        S.op("pe", "matmul", reads=[b_P[pi], b_Vx[ss]], writes=[b_ps[po]], inc=(g == 3 and idx == 1),
                             out=pov[:, g, 0:65], lhsT=Pb[pi][:, g * 128:(g + 1) * 128], rhs=Vx[:, ss, kvh, :],
                             start=(idx == 0), stop=(idx == 1))
                sd = smr.next()
                hq0 = j * 8 + kk * 4
                S.op("dve", "tensor_tensor", reads=[b_ps[po], b_esink], writes=[b_sm[sd]], out=sm[sd][:, 0:4].unsqueeze(2),
                     in0=pov[:, :, 64:65], in1=esink[:, hq0:hq0 + 4].unsqueeze(2), op=ALU.add)
                S.op("dve", "reciprocal", reads=[b_sm[sd]], writes=[b_sm[sd]], out=sm[sd][:, 0:4], in_=sm[sd][:, 0:4])
                S.op("dve", "tensor_tensor", reads=[b_ps[po], b_sm[sd]], writes=[b_atm[ai]],
                     out=atm[ai][:, kk * 256:(kk + 1) * 256].rearrange("p (g d) -> p g d", g=4), in0=pov[:, :, 0:64],
                     in1=sm[sd][:, 0:4].unsqueeze(2).to_broadcast([128, 4, 64]), op=ALU.mult)
            yield
            pt = psB.next()
            ptb = psum[pt][:].bitcast(BF16)
            for cc in range(4):
                S.op("pe", "transpose", reads=[b_atm[ai], b_identbf], writes=[b_ps[pt]], inc=(cc == 3),
                     out=ptb[:, cc * 128:(cc + 1) * 128], in_=atm[ai][:, cc * 128:(cc + 1) * 128], identity=ident_bf[:])
            k = colof[b] // 128
            evac_copy("act", cb[:, AT + j * 4:AT + j * 4 + 4, colof[b]:colof[b] + 128],
                      ptb[:, 0:512].rearrange("p (c n) -> p c n", c=4), [b_ps[pt]], [b_cb[(AT + j * 4 + cc, k)] for cc in range(4)])

        ust = {"q1done": False, "pend": 0}
        uslabs = {}

        def u_item(j, oc, ti, firstj, lastj):
            if firstj:
                uslabs[j] = W.get()
            sl = uslabs[j]
            wvv = wv(sl, 0, 16, 256)
            g = j * 2 + oc
            off, n = tt[ti]
            p = psB.next()
            for c in range(NCH):
                S.op("pe", "matmul", reads=cbb(HT + c, off, n) + wb(sl), writes=[b_ps[p]], inc=(c == NCH - 1),
                     out=psum[p][:, 0:n], lhsT=wvv[:, c, oc * 128:(oc + 1) * 128], rhs=cbv(HT + c, off, n),
                     start=(c == 0), stop=(c == NCH - 1))
            gelu_tanh(None, psum[p][:, 0:n], [b_ps[p]], cbv(GT + g, off, n), cbb(GT + g, off, n), n)
            if lastj:
                if ust["q1done"]:
                    W.done(1)
                else:
                    ust["pend"] += 1
            return
            yield

        def u_items(js):
            res = []
            for j in js:
                grp = [(oc, ti) for oc in range(2) for ti in range(len(tt))]
                for idx, (oc, ti) in enumerate(grp):
                    res.append(u_item(j, oc, ti, idx == 0, idx == len(grp) - 1))
            return res

        items = [kv_item(b, b == kvblocks[-1]) for b in kvblocks]
        items += [attn_item(0, b, b == F[0], b == F[-1]) for b in F]
        a1 = [attn_item(1, b, b == F[0], b == F[-1]) for b in F]
        u01, u23 = u_items([0, 1]), u_items([2, 3])
        mixed = [a1[0]]
        ui = 0
        for it_ in a1[1:]:
            per = -(-len(u01) // (len(a1) - 1))
            mixed += u01[ui:ui + per]
            ui += per
            mixed.append(it_)
        mixed += u01[ui:]
        items += mixed + u23
        run_pipeline(items)
        dump(f"kT_{wave}_{l}", kT[0:64], b_kT, [64, 7, 4, 128], BF16)
        dump(f"Vx_{wave}_{l}", Vx[:], b_Vx, [128, 7, 4, 65], BF16)
        dump(f"attnT_{wave}_{l}", cb[:, AT:AT + 8, :], [b_cb[(AT + c, k)] for c in range(8) for k in range(5)], [128, 8, TW], BF16)
        if stop_after == f"attn_{wave}_{l}":
            return False

        vslabs = {}

        def sgu_item(j, b, firstj, lastj):
            if firstj:
                vslabs[j] = (W.get(), W.get())
                S.dma("sp", None, lngb[:], lngb_in[l, :, j * 512:(j + 1) * 512].partition_broadcast(128), writes=[b_lngb])
            sl_a, sl_b = vslabs[j]
            if True:
                p = psA.next()
                tm_proj(b, [sl_a, sl_b], p)
                if lastj:
                    W.done(2)
                yield
                tg = tmr.next()
                vg = tm[tg][:, 0:512]
                gelu_tanh(None, psum[p][:, 0:512], [b_ps[p]], vg, [b_tm[tg]], 512)
                t2 = tmr.next()
                s1, s2 = smr.next(), smr.next()
                vg3 = vg.rearrange("p (g d) -> p g d", g=4)
                S.op("dve", "tensor_reduce", reads=[b_tm[tg]], writes=[b_sm[s1]], out=sm[s1][:, 0:4], in_=vg3, axis=AX.X, op=ALU.add)
                S.op("act", "activation", reads=[b_tm[tg]], writes=[b_tm[t2]], out=tm[t2][:, 0:512], in_=vg, func=AF.Square)
                S.op("dve", "tensor_reduce", reads=[b_tm[t2]], writes=[b_sm[s2]], out=sm[s2][:, 0:4],
                     in_=tm[t2][:, 0:512].rearrange("p (g d) -> p g d", g=4), axis=AX.X, op=ALU.add)
                S.op("dve", "tensor_scalar", reads=[b_sm[s1]], writes=[b_sm[s1]], out=sm[s1][:, 0:4], in0=sm[s1][:, 0:4],
                     scalar1=1.0 / 128, scalar2=None, op0=ALU.mult)
                S.op("dve", "tensor_tensor", reads=[b_sm[s1]], writes=[b_sm[s1]], out=sm[s1][:, 4:8], in0=sm[s1][:, 0:4],
                     in1=sm[s1][:, 0:4], op=ALU.mult)
                S.op("dve", "scalar_tensor_tensor", reads=[b_sm[s2], b_sm[s1]], writes=[b_sm[s2]], out=sm[s2][:, 0:4],
                     in0=sm[s2][:, 0:4], scalar=1.0 / 128, in1=sm[s1][:, 4:8], op0=ALU.mult, op1=ALU.subtract)
                S.op("act", "activation", reads=[b_sm[s2], b_eps], writes=[b_sm[s2]], out=sm[s2][:, 0:4], in_=sm[s2][:, 0:4],
                     func=AF.Ln, bias=epst[:, 0:1], scale=1.0)
                S.op("act", "activation", reads=[b_sm[s2]], writes=[b_sm[s2]], out=sm[s2][:, 0:4], in_=sm[s2][:, 0:4],
                     func=AF.Exp, scale=-0.5)
                S.op("dve", "tensor_tensor", reads=[b_tm[tg], b_sm[s1]], writes=[b_tm[tg]], out=vg3, in0=vg3,
                     in1=sm[s1][:, 0:4].unsqueeze(2).to_broadcast([128, 4, 128]), op=ALU.subtract)
                S.op("dve", "tensor_tensor", reads=[b_tm[tg], b_sm[s2]], writes=[b_tm[tg]], out=vg3, in0=vg3,
                     in1=sm[s2][:, 0:4].unsqueeze(2).to_broadcast([128, 4, 128]), op=ALU.mult)
                S.op("dve", "tensor_tensor", reads=[b_tm[tg], b_lngb], writes=[b_tm[tg]], out=vg, in0=vg, in1=lngb[:, 0, :], op=ALU.mult)
                vi = vnr.next()
                S.op("dve", "tensor_tensor", reads=[b_tm[tg], b_lngb], writes=[b_vn[vi]], out=vnb[vi][:], in0=vg, in1=lngb[:, 1, :], op=ALU.add)
                yield
                pspt = psB.next()
                for g4 in range(4):
                    g = j * 4 + g4
                    o_ = psum[pspt][:, g4 * 128:(g4 + 1) * 128]
                    S.op("pe", "matmul", reads=[b_vn[vi], b_WsT], writes=[b_ps[pspt]], inc=False, out=o_,
                         lhsT=vnb[vi][:, g4 * 128:(g4 + 1) * 128], rhs=WsT[:, g, :], start=True, stop=False)
                    S.op("pe", "matmul", reads=[b_ones, b_bsp], writes=[b_ps[pspt]], inc=False, out=o_,
                         lhsT=ones_bf[0:1, :], rhs=bsp_hi[0:1, g * 128:(g + 1) * 128], start=False, stop=False)
                    S.op("pe", "matmul", reads=[b_ones, b_bsp], writes=[b_ps[pspt]], inc=(g4 == 3), out=o_,
                         lhsT=ones_bf[0:1, :], rhs=bsp_lo[0:1, g * 128:(g + 1) * 128], start=False, stop=True)
                k = colof[b] // 128
                gsl = cb[:, GT + j * 4:GT + j * 4 + 4, colof[b]:colof[b] + 128]
                if "sgudbg_u" in dumps:
                    S.op("dve", "tensor_copy", reads=[b_ps[pspt]], writes=[b_sm[0]], out=sm[0][:, 0:1], in_=psum[pspt][:, 0:1])
                elif "sgudbg_s" in dumps:
                    S.op("dve", "tensor_copy", reads=[b_ps[pspt]], writes=[b_cb[(GT + j * 4 + g4, k)] for g4 in range(4)], out=gsl,
                         in_=psum[pspt][:].rearrange("p (g i) -> p g i", g=4))
                else:
                    S.op("dve", "tensor_tensor", reads=[b_ps[pspt]] + [b_cb[(GT + j * 4 + g4, k)] for g4 in range(4)],
                         writes=[b_cb[(GT + j * 4 + g4, k)] for g4 in range(4)], out=gsl,
                         in0=psum[pspt][:].rearrange("p (g i) -> p g i", g=4), in1=gsl, op=ALU.mult)

        run_pipeline([sgu_item(j, b, b == F[0], b == F[-1]) for j in range(2) for b in F])
        dump(f"gatedT_{wave}_{l}", cb[:, GT:GT + 8, :], [b_cb[(GT + c, k)] for c in range(8) for k in range(5)], [128, 8, TW], BF16)
        if stop_after == f"sgu_{wave}_{l}":
            return False

        def fm_group(p, n, off, pieces):
            last = len(pieces) - 1
            for i, (lt, rh, rb) in enumerate(pieces):
                S.op("pe", "matmul", reads=rb, writes=[b_ps[p]], inc=(i == last), out=psum[p][:, 0:n], lhsT=lt, rhs=rh,
                     start=(i == 0), stop=(i == last))

        for s in range(8):
            s2, s1 = W.get(), W.get()
            wa, wsg = wv(s1, 0, 8, 256), wv(s1, 2048, 8, 256)
            wga = wv(s2, 0, 16, 256)
            for oc in range(2):
                o = s * 2 + oc
                osl = slice(oc * 128, (oc + 1) * 128)
                for (off, n) in tt:
                    pa, pga = psr.next(), psr.next()
                    fm_group(pa, n, off, [(wa[:, c, osl], cbv(AT + c, off, n), cbb(AT + c, off, n) + wb(s1)) for c in range(8)])
                    fm_group(pga, n, off, [(wga[:, c, osl], cbv(HT + c, off, n), cbb(HT + c, off, n) + wb(s2)) for c in range(16)])
                    ta = tmr.next()
                    S.op("act", "activation", reads=[b_ps[pga]], writes=[b_tm[ta]], out=tm[ta][:, 0:n], in_=psum[pga][:, 0:n], func=AF.Sigmoid)
                    S.op("dve", "tensor_tensor", reads=[b_ps[pa], b_tm[ta]], writes=cbb(MG + o, off, n), out=cbv(MG + o, off, n),
                         in0=psum[pa][:, 0:n], in1=tm[ta][:, 0:n], op=ALU.mult)
            W.done(1)
            s3 = W.get()
            wgb = wv(s3, 0, 16, 256)
            for oc in range(2):
                o = s * 2 + oc
                osl = slice(oc * 128, (oc + 1) * 128)
                for (off, n) in tt:
                    pb, pgb = psr.next(), psr.next()
                    fm_group(pb, n, off, [(wsg[:, c, osl], cbv(GT + c, off, n), cbb(GT + c, off, n) + wb(s1)) for c in range(8)])
                    fm_group(pgb, n, off, [(wgb[:, c, osl], cbv(HT + c, off, n), cbb(HT + c, off, n) + wb(s3)) for c in range(16)])
                    tb = tmr.next()
                    S.op("act", "activation", reads=[b_ps[pgb]], writes=[b_tm[tb]], out=tm[tb][:, 0:n], in_=psum[pgb][:, 0:n], func=AF.Sigmoid)
                    S.op("dve", "tensor_tensor", reads=[b_ps[pb], b_tm[tb]], writes=[b_tm[tb]], out=tm[tb][:, 0:n], in0=psum[pb][:, 0:n],
                         in1=tm[tb][:, 0:n], op=ALU.mult)
                    S.op("dve", "tensor_tensor", reads=[b_tm[tb]] + cbb(MG + o, off, n), writes=cbb(MG + o, off, n), out=cbv(MG + o, off, n),
                         in0=tm[tb][:, 0:n], in1=cbv(MG + o, off, n), op=ALU.add)
            W.done(2)
        dump(f"mergedT_{wave}_{l}", cb[:, MG:MG + 16, :], [b_cb[(MG + c, k)] for c in range(16) for k in range(5)], [128, 16, TW], BF16)

        sq_m3 = StatsQ()
        for s in range(8):
            sl = W.get()
            wvv = wv(sl, 0, 16, 256)
            for oc in range(2):
                o = s * 2 + oc
                for ti, (off, n) in enumerate(tt):
                    p = psB.next()
                    fm_group(p, n, off, [(wvv[:, c, oc * 128:(oc + 1) * 128], cbv(MG + c, off, n), cbb(MG + c, off, n) + wb(sl))
                                         for c in range(16)])
                    S.op("dve", "tensor_tensor", reads=[b_ps[p]] + xTb(o, off, n), writes=xTb(o, off, n), out=xT[:, o, off:off + n],
                         in0=psum[p][:, 0:n], in1=xT[:, o, off:off + n], op=ALU.add)
                    sq_m3.push(o, off, n, ti, o == 0, o == 15)
            W.done(1)
        sq_m3.flush()
        dump(f"xmix_{wave}_{l}", xT[:], list(b_xT.values()), [128, 16, TW], F32)
        if stop_after == f"mix_{wave}_{l}":
            return False

        rmsnorm_cols(gffn, tt, pre=list(range(len(tt))))
        for (fc0, nfc) in FFN_PARTS:
            for jj in range(nfc // 2):
                sg_, su_ = W.get(), W.get()
                wg, wu = wv(sg_, 0, 16, 256), wv(su_, 0, 16, 256)
                for oc in range(2):
                    a_i = ACT0 + jj * 2 + oc
                    osl = slice(oc * 128, (oc + 1) * 128)
                    for (off, n) in tt:
                        pg, pu = psr.next(), psr.next()
                        fm_group(pg, n, off, [(wg[:, c, osl], cbv(HT + c, off, n), cbb(HT + c, off, n) + wb(sg_)) for c in range(16)])
                        fm_group(pu, n, off, [(wu[:, c, osl], cbv(HT + c, off, n), cbb(HT + c, off, n) + wb(su_)) for c in range(16)])
                        ts = tmr.next()
                        S.op("act", "activation", reads=[b_ps[pg]], writes=[b_tm[ts]], out=tm[ts][:, 0:n], in_=psum[pg][:, 0:n], func=AF.Silu)
                        S.op("dve", "tensor_tensor", reads=[b_ps[pu], b_tm[ts]], writes=cbb(a_i, off, n), out=cbv(a_i, off, n),
                             in0=psum[pu][:, 0:n], in1=tm[ts][:, 0:n], op=ALU.mult)
                W.done(2)
            sq_ffn = StatsQ()
            for s in range(8):
                sl = W.get()
                wvv = wv(sl, 0, nfc, 256)
                carry_stats = (l == 0 and fc0 == FFN_PARTS[-1][0] and stop_after is None)
                for oc in range(2):
                    o = s * 2 + oc
                    for ti, (off, n) in enumerate(tt):
                        p = psB.next() if carry_stats else psr.next()
                        fm_group(p, n, off, [(wvv[:, c, oc * 128:(oc + 1) * 128], cbv(ACT0 + c, off, n), cbb(ACT0 + c, off, n) + wb(sl))
                                             for c in range(nfc)])
                        S.op("dve", "tensor_tensor", reads=[b_ps[p]] + xTb(o, off, n), writes=xTb(o, off, n), out=xT[:, o, off:off + n],
                             in0=psum[p][:, 0:n], in1=xT[:, o, off:off + n], op=ALU.add)
                        if carry_stats:
                            sq_ffn.push(o, off, n, ti, o == 0, o == 15)
                W.done(1)
            sq_ffn.flush()
            if l == 0 and fc0 == FFN_PARTS[-1][0] and stop_after is None:
                carry["pre"] = list(range(len(tt)))
        dump(f"xout_{wave}_{l}", xT[:], list(b_xT.values()), [128, 16, TW], F32)
        if stop_after == f"layer_{wave}_{l}":
            return False

        if l == 1:
            for b in F:
                for hf in range(2):
                    xi = xir.next()
                    for g in range(2):
                        p = psr.next()
                        for jx in range(4):
                            c = hf * 8 + g * 4 + jx
                            S.op("pe", "transpose", reads=[b_xT[(c, colof[b] // 128)], b_cf], writes=[b_ps[p]], inc=(jx == 3),
                                 out=psum[p][:, jx * 128:(jx + 1) * 128], in_=xT[:, c, colof[b]:colof[b] + 128], identity=ident)
                        evac_copy(evr.next(), xin[xi][:, g * 512:(g + 1) * 512], psum[p][:], [b_ps[p]], [b_xin[xi]])
                    out_events.append(S.dma("sp", None, out_d[(b - 2) * 128:(b - 1) * 128, hf * 1024:(hf + 1) * 1024], xin[xi][:],
                                            reads=[b_xin[xi]]))
        return True

    ok = True
    for wave in range(2):
        for l in range(2):
            if ok:
                ok = layer(wave, l)
    for ev in out_events:
        S.wait_event("sp", ev)
    S.emit()
    return nc, S


def _consts():
    ident = np.eye(128, dtype=np.float32)
    j = np.arange(128)[:, None]
    i = np.arange(128)[None, :]
    tri = (j <= i).astype(np.float32)
    cur = np.where(j <= i, 0.0, NEG).astype(np.float32)
    prev = np.where(j > i, 0.0, NEG).astype(np.float32)
    return ident, tri, np.tile(cur, (1, 4)), np.tile(prev, (1, 4))


def _rope_tables():
    pos = np.arange(2048, dtype=np.float32)
    inv = np.power(np.float32(10000.0), -np.arange(0, 64, 2, dtype=np.float32) / np.float32(64)).astype(np.float32)
    ang = (pos[:, None] * inv[None, :]).astype(np.float32)
    return np.cos(ang).astype(np.float32), np.sin(ang).astype(np.float32)


def make_in_maps(inputs):
    x = np.ascontiguousarray(inputs["x"], dtype=np.float32)
    ident, tri, cur, prev = _consts()
    cos, sin = _rope_tables()
    cf = np.concatenate([ident, tri], axis=1)
    nrm = np.zeros((64, 128), np.float32)
    for l in range(2):
        nrm[l * 32:l * 32 + 16] = inputs["mix_norm"][l].reshape(16, 128)
        nrm[l * 32 + 16:l * 32 + 32] = inputs["ffn_norm"][l].reshape(16, 128)
    bc = np.concatenate([inputs["q_norm"], inputs["k_norm"], inputs["sinks"]], axis=1).astype(np.float32)
    lngb = np.stack([inputs["sgu_ln_g"], inputs["sgu_ln_b"]], axis=1).astype(np.float32)
    bsp = inputs["b_spatial"].reshape(2, 1024).astype(np.float32)
    shared = {
        "cf": cf, "nrm": nrm, "bc": bc, "lngb": lngb, "bsp": bsp,
        "w_spatial": np.ascontiguousarray(inputs["w_spatial"], dtype=np.float32),
        "w_in": np.ascontiguousarray(inputs["w_in"], dtype=np.float32),
        "w_attn": np.ascontiguousarray(inputs["w_attn_branch"], dtype=np.float32),
        "w_sgu": np.ascontiguousarray(inputs["w_sgu_branch"], dtype=np.float32),
        "w_out": np.ascontiguousarray(inputs["w_out"], dtype=np.float32),
        "w_gate": np.ascontiguousarray(inputs["w_gate"], dtype=np.float32),
        "w_up": np.ascontiguousarray(inputs["w_up"], dtype=np.float32),
        "w_down": np.ascontiguousarray(inputs["w_down"], dtype=np.float32),
    }
    maps = []
    for core in range(8):
        b, h = core // 2, core % 2
        start = h * 1024
        xc = np.zeros((1280, D), np.float32)
        csc = np.zeros((1280, 64), np.float32)
        lo = start - 256
        src0 = max(lo, 0)
        xc[src0 - lo:] = x[b, src0:start + 1024]
        csc[src0 - lo:, 0:32] = cos[src0:start + 1024]
        csc[src0 - lo:, 32:64] = sin[src0:start + 1024]
        first = np.full_like(prev, NEG) if h == 0 else prev
        cm = np.concatenate([cur, prev, first], axis=1)
        m = dict(shared)
        m.update({"x": xc, "cs": csc, "cm": cm})
        maps.append(m)
    return maps


_CACHE = {}


def kernel(**inputs):
    inputs = {k: np.asarray(v) for k, v in inputs.items()}
    if "nc" not in _CACHE:
        _CACHE["nc"] = build()[0]
    nc = _CACHE["nc"]
    maps = make_in_maps(inputs)
    res = run_bass_kernel_spmd(nc, maps, core_ids=list(range(8)))
    out = np.zeros((4, 2048, D), np.float32)
    for core in range(8):
        b, h = core // 2, core % 2
        out[b, h * 1024:(h + 1) * 1024] = res.results[core]["out"]
    return out
```

```python
import numpy as np
import concourse.bass as bass
import concourse.mybir as mybir
from concourse.bass_utils import run_bass_kernel_spmd

AF = mybir.ActivationFunctionType
ALU = mybir.AluOpType
AX = mybir.AxisListType
F32 = mybir.dt.float32
BF16 = mybir.dt.bfloat16

ENGS = ("pe", "act", "dve", "pool", "sp")
EPS = 1e-6
D = 2048
NCH = 16
DFF = 5632
KOFF, VOFF, UOFF, VSOFF, GAOFF, GBOFF = 1024, 1280, 1536, 2560, 3584, 5632
NEG = -30000.0
TW = 640
BCW = 144
FFN_PARTS = [(0, 12), (12, 12), (24, 12), (36, 8)]


class Buf:
    __slots__ = ("name", "w", "r", "excl")

    def __init__(self, name, excl=False):
        self.name = name
        self.w = None
        self.r = {}
        self.excl = excl


class Sched:
    def __init__(self, nc, same_eng_sync=True):
        self.nc = nc
        self.q = {e: [] for e in ENGS}
        self.semh = {}
        self.cnt = {}
        for e in ENGS:
            self.semh[e] = nc.alloc_semaphore("s_" + e)
            self.cnt[e] = 0
        self.seen = {e: {} for e in ENGS}
        self.same_eng_sync = same_eng_sync
        self.pe_pending = False
        self.nwaits = 0

    def _need(self, eng, ev, needs):
        key, val = ev
        if key == eng and (eng == "pe" or not self.same_eng_sync):
            return
        if self.seen[eng].get(key, 0) >= val:
            return
        if needs.get(key, 0) < val:
            needs[key] = val

    def _deps(self, eng, reads, writes):
        needs = {}
        for b in reads:
            if b.w is not None:
                self._need(eng, b.w, needs)
            if b.excl:
                for k, v in b.r.items():
                    if k != eng:
                        self._need(eng, (k, v), needs)
        for b in writes:
            if b.w is not None:
                self._need(eng, b.w, needs)
            for k, v in b.r.items():
                if k == eng:
                    continue
                self._need(eng, (k, v), needs)
        for key, val in needs.items():
            self.seen[eng][key] = val
            h = self.semh[key]
            self.nwaits += 1
            self.q[eng].append(lambda e, h=h, val=val: e.wait_ge(h, val))

    def _mark(self, ev, reads, writes):
        k, v = ev
        for b in reads:
            if b.r.get(k, 0) < v:
                b.r[k] = v
        for b in writes:
            b.w = ev
            b.r = {}

    def op(self, eng, meth, reads=(), writes=(), inc=True, **kw):
        self._deps(eng, reads, writes)
        if inc:
            self.cnt[eng] += 1
            ev = (eng, self.cnt[eng])
            h = self.semh[eng]
            self.q[eng].append(lambda e, meth=meth, kw=kw, h=h: getattr(e, meth)(**kw).then_inc(h, 1))
            if eng == "pe":
                self.pe_pending = False
        else:
            assert eng == "pe"
            ev = (eng, self.cnt[eng] + 1)
            self.q[eng].append(lambda e, meth=meth, kw=kw: getattr(e, meth)(**kw))
            self.pe_pending = True
        self._mark(ev, reads, writes)
        return ev

    def dma(self, eng, key, out, in_, reads=(), writes=()):
        if key is None:
            key = "b_" + (writes[0].name if len(writes) else reads[0].name)
        if key not in self.semh:
            self.semh[key] = self.nc.alloc_semaphore("d_" + key)
            self.cnt[key] = 0
        self._deps(eng, reads, writes)
        self.cnt[key] += 16
        ev = (key, self.cnt[key])
        h = self.semh[key]
        self.q[eng].append(lambda e, out=out, in_=in_, h=h: e.dma_start(out=out, in_=in_).then_inc(h, 16))
        self._mark(ev, reads, writes)
        return ev

    def wait_event(self, eng, ev):
        key, val = ev
        if self.seen[eng].get(key, 0) < val:
            self.seen[eng][key] = val
            h = self.semh[key]
            self.q[eng].append(lambda e, h=h, val=val: e.wait_ge(h, val))

    def emit(self):
        assert not self.pe_pending
        q = self.q
        with self.nc.Block() as block:
            @block.tensor
            def _(e):
                for f in q["pe"]:
                    f(e)

            @block.scalar
            def _(e):
                for f in q["act"]:
                    f(e)

            @block.vector
            def _(e):
                for f in q["dve"]:
                    f(e)

            @block.gpsimd
            def _(e):
                for f in q["pool"]:
                    f(e)

            @block.sync
            def _(e):
                for f in q["sp"]:
                    f(e)


class Ring:
    def __init__(self, items):
        self.items = items
        self.i = 0

    def next(self):
        it = self.items[self.i % len(self.items)]
        self.i += 1
        return it


def build(stop_after=None, dumps=()):
    nc = bass.Bass("TRN2", target_bir_lowering=False)
    S = Sched(nc)
    dumps = set(dumps)

    x_in = nc.dram_tensor("x", [1280, D], F32, kind="ExternalInput").ap()
    cs_in = nc.dram_tensor("cs", [1280, 64], F32, kind="ExternalInput").ap()
    cf_in = nc.dram_tensor("cf", [128, 256], F32, kind="ExternalInput").ap()
    cm_in = nc.dram_tensor("cm", [128, 1536], F32, kind="ExternalInput").ap()
    nrm_in = nc.dram_tensor("nrm", [64, 128], F32, kind="ExternalInput").ap()
    bc_in = nc.dram_tensor("bc", [2, BCW], F32, kind="ExternalInput").ap()
    lngb_in = nc.dram_tensor("lngb", [2, 2, 1024], F32, kind="ExternalInput").ap()
    bsp_in = nc.dram_tensor("bsp", [2, 1024], F32, kind="ExternalInput").ap()
    wsp_in = nc.dram_tensor("w_spatial", [2, 8, 128, 128], F32, kind="ExternalInput").ap()
    w_in = nc.dram_tensor("w_in", [2, D, 7680], F32, kind="ExternalInput").ap()
    w_attn = nc.dram_tensor("w_attn", [2, 1024, D], F32, kind="ExternalInput").ap()
    w_sgu = nc.dram_tensor("w_sgu", [2, 1024, D], F32, kind="ExternalInput").ap()
    w_out = nc.dram_tensor("w_out", [2, D, D], F32, kind="ExternalInput").ap()
    w_gate = nc.dram_tensor("w_gate", [2, D, DFF], F32, kind="ExternalInput").ap()
    w_up = nc.dram_tensor("w_up", [2, D, DFF], F32, kind="ExternalInput").ap()
    w_down = nc.dram_tensor("w_down", [2, DFF, D], F32, kind="ExternalInput").ap()
    out_d = nc.dram_tensor("out", [1024, D], F32, kind="ExternalOutput").ap()

    def sb(name, shape, dt):
        return nc.alloc_sbuf_tensor("sb_" + name, shape, dt)

    xT = sb("xT", [128, NCH, TW], F32)
    cb = sb("cb", [128, 48, TW], BF16)
    h0T = sb("h0T", [128, NCH, 128], BF16)
    NSL = 4
    wsl = [sb(f"wsl{i}", [128, 4096], BF16) for i in range(NSL)]
    kT = sb("kT", [128, 7, 4, 128], BF16)
    Vx = sb("Vx", [128, 7, 4, 65], BF16)
    xin = [sb(f"xin{i}", [128, 1024], F32) for i in range(2)]
    cf = sb("cf", [128, 256], F32)
    cmb = sb("cmb", [128, 1536], BF16)
    ident_bf = sb("ident_bf", [128, 128], BF16)
    ones_bf = sb("ones_bf", [128, 128], BF16)
    epst = sb("epst", [128, 1], F32)
    nrm_sb = sb("nrm_sb", [64, 128], F32)
    gT = sb("gT", [128, 64], F32)
    cs_sb = sb("cs_sb", [128, 10, 64], F32)
    bcl = sb("bcl", [128, BCW], F32)
    esink = sb("esink", [128, 16], F32)
    lngb = sb("lngb", [128, 2, 512], F32)
    WsT = sb("WsT", [128, 8, 128], BF16)
    bsp_hi = sb("bsp_hi", [1, 1024], BF16)
    bsp_lo = sb("bsp_lo", [1, 1024], BF16)
    sqb = [sb(f"sq{i}", [128, 512], BF16) for i in range(3)]
    rstd = sb("rstd", [128, 512], F32)
    NTM = 5
    tm = [sb(f"tm{i}", [128, 512], F32) for i in range(NTM)]
    sm = [sb(f"sm{i}", [128, 16], F32) for i in range(8)]
    qrot = [sb(f"qrot{i}", [128, 512], BF16) for i in range(2)]
    qTb = [sb(f"qT{i}", [128, 8, 128], BF16) for i in range(2)]
    Pb = [sb(f"P{i}", [128, 512], BF16) for i in range(8)]
    atm = [sb(f"atm{i}", [128, 512], BF16) for i in range(2)]
    vnb = [sb(f"vn{i}", [128, 512], BF16) for i in range(2)]
    psum = [nc.alloc_psum_tensor(f"ps{i}", [128, 512], F32) for i in range(8)]

    ident = cf[:, 0:128]
    tri = cf[:, 128:256]
    maskcur = cmb[:, 0:512]
    maskprev = cmb[:, 512:1024]
    maskfirst = cmb[:, 1024:1536]

    B = Buf
    b_xT = {(c, k): B(f"xT{c}_{k}") for c in range(NCH) for k in range(5)}
    b_cb = {(i, k): B(f"cb{i}_{k}") for i in range(48) for k in range(5)}
    b_h0T = B("h0T")
    b_wsl = [B(f"wsl{i}") for i in range(NSL)]
    b_wslB = [B(f"wslB{i}") for i in range(NSL)]

    def wb(sl):
        return [b_wsl[sl], b_wslB[sl]]
    b_kT = [B(f"kT{i}") for i in range(7)]
    b_Vx = [B(f"Vx{i}") for i in range(7)]
    b_xin = [B("xin0"), B("xin1")]
    b_cf, b_cmb, b_identbf, b_ones, b_eps, b_nrm, b_gT, b_cs = (B("cf"), B("cmb"), B("identbf"), B("ones"),
                                                                 B("eps"), B("nrm"), B("gT"), B("cs"))
    b_bcl, b_esink, b_lngb, b_WsT, b_bsp = B("bcl"), B("esink"), B("lngb"), B("WsT"), B("bsp")
    b_sq = [B(f"sq{i}") for i in range(3)]
    b_rstd = B("rstd")
    b_tm = [B(f"tm{i}") for i in range(NTM)]
    b_sm = [B(f"sm{i}") for i in range(8)]
    b_qrot = [B("qrot0"), B("qrot1")]
    b_qT = [B("qT0"), B("qT1")]
    b_P = [B(f"P{i}") for i in range(8)]
    b_atm = [B("atm0"), B("atm1")]
    b_vn = [B("vn0"), B("vn1")]
    b_ps = [B(f"ps{i}", excl=True) for i in range(8)]

    psr = Ring(list(range(8)))
    psA = Ring([0, 1])
    psB = Ring([2, 3, 4, 5, 6, 7])
    sqr = Ring(list(range(3)))
    tmr = Ring(list(range(NTM)))
    smr = Ring(list(range(8)))
    qrr = Ring([0, 1])
    qtr = Ring([0, 1])
    Pr = Ring(list(range(8)))
    atr = Ring([0, 1])
    vnr = Ring([0, 1])
    xir = Ring([0, 1])
    evr = Ring(["act", "dve"])

    out_events = []
    dump_tensors = {}

    def dump(name, ap, bufs, shape, dt):
        if name not in dumps:
            return
        t = nc.dram_tensor("dbg_" + name, list(shape), dt, kind="ExternalOutput").ap()
        out_events.append(S.dma("sp", "dbg_" + name, t, ap, reads=bufs))
        dump_tensors[name] = True

    HT, AT, GT, MG, ACT0 = 0, 16, 24, 32, 16

    def cbv(i, off, n):
        return cb[:, i, off:off + n]

    def blocks_of(off, n):
        return list(range(off // 128, (off + n + 127) // 128))

    def cbb(i, off, n):
        return [b_cb[(i, k)] for k in blocks_of(off, n)]

    def xTb(c, off, n):
        return [b_xT[(c, k)] for k in blocks_of(off, n)]

    class WStream:
        def __init__(self):
            self.plan = []
            self.next_issue = 0
            self.next_get = 0

        def add(self, pieces):
            self.plan.append(pieces)

        def _issue(self, i):
            slot = i % NSL
            for (src, off, nch, ncols) in self.plan[i]:
                dst = wsl[slot][:, off:off + nch * ncols].rearrange("p (c n) -> p c n", c=nch)
                if nch * ncols > 2048:
                    wr = [b_wsl[slot], b_wslB[slot]]
                else:
                    wr = [b_wsl[slot]][:] if off == 0 else [b_wslB[slot]]
                S.dma("pool", None, dst, src, writes=wr)

        def start(self):
            while self.next_issue < min(NSL, len(self.plan)):
                self._issue(self.next_issue)
                self.next_issue += 1

        def get(self):
            i = self.next_get
            self.next_get += 1
            assert i < self.next_issue, "slab not issued (ring too small for access pattern)"
            return i % NSL

        def done(self, n=1):
            for _ in range(n):
                if self.next_issue < len(self.plan):
                    self._issue(self.next_issue)
                    self.next_issue += 1

    W = WStream()

    def wsrc(w, l, r0, nch, c0, ncols):
        return (w[l, r0:r0 + nch * 128, c0:c0 + ncols].rearrange("(c p) n -> p c n", p=128), nch, ncols)

    def plan_layer(l):
        def one(w, r0, nch, c0, ncols, off=0):
            src, nch_, ncols_ = wsrc(w, l, r0, nch, c0, ncols)
            return (src, off, nch_, ncols_)
        for c0 in (KOFF, VOFF):
            W.add([one(w_in, 0, 16, c0, 256)])
        for j in range(4):
            W.add([one(w_in, 0, 16, j * 256, 256)])
        for j in range(4):
            W.add([one(w_in, 0, 16, UOFF + j * 256, 256)])
        for j in range(4):
            W.add([one(w_in, 0, 16, VSOFF + j * 256, 256)])
        for s in range(8):
            W.add([one(w_in, 0, 16, GAOFF + s * 256, 256)])
            W.add([one(w_attn, 0, 8, s * 256, 256, 0), one(w_sgu, 0, 8, s * 256, 256, 2048)])
            W.add([one(w_in, 0, 16, GBOFF + s * 256, 256)])
        for s in range(8):
            W.add([one(w_out, 0, 16, s * 256, 256)])
        for (fc0, nfc) in FFN_PARTS:
            for j in range(nfc // 2):
                W.add([one(w_gate, 0, 16, (fc0 + 2 * j) * 128, 256)])
                W.add([one(w_up, 0, 16, (fc0 + 2 * j) * 128, 256)])
            for s in range(8):
                W.add([one(w_down, fc0 * 128, nfc, s * 256, 256)])

    def wv(slot, off, nch, ncols):
        return wsl[slot][:, off:off + nch * ncols].rearrange("p (c n) -> p c n", c=nch)

    def evac_copy(eng, out, in_, reads, writes):
        if eng == "act":
            S.op("act", "activation", reads=reads, writes=writes, out=out, in_=in_, func=AF.Copy)
        else:
            S.op(eng, "tensor_copy", reads=reads, writes=writes, out=out, in_=in_)

    def rms_stats(src_fn, n, src_bufs_fn):
        p = psr.next()
        for c in range(NCH):
            s = sqr.next()
            S.op("act", "activation", reads=src_bufs_fn(c), writes=[b_sq[s]],
                 out=sqb[s][:, 0:n], in_=src_fn(c), func=AF.Square)
            S.op("pe", "matmul", reads=[b_sq[s], b_ones], writes=[b_ps[p]], inc=True,
                 out=psum[p][:, 0:n], lhsT=ones_bf[:], rhs=sqb[s][:, 0:n], start=(c == 0), stop=(c == NCH - 1))
        S.op("act", "activation", reads=[b_ps[p], b_eps], writes=[b_rstd],
             out=rstd[:, 0:n], in_=psum[p][:, 0:n], func=AF.Ln, bias=epst[:, 0:1], scale=1.0 / D)
        S.op("act", "activation", reads=[b_rstd], writes=[b_rstd], out=rstd[:, 0:n], in_=rstd[:, 0:n], func=AF.Exp, scale=-0.5)

    def rmsnorm_cols(gcol0, tiles):
        for (off, n) in tiles:
            rms_stats(lambda c: xT[:, c, off:off + n], n, lambda c: xTb(c, off, n))
            for c in range(NCH):
                S.op("dve", "scalar_tensor_tensor", reads=xTb(c, off, n) + [b_rstd, b_gT], writes=cbb(HT + c, off, n),
                     out=cbv(HT + c, off, n), in0=xT[:, c, off:off + n], scalar=gT[:, gcol0 + c:gcol0 + c + 1],
                     in1=rstd[:, 0:n], op0=ALU.mult, op1=ALU.mult)

    def head_norm_rope(p, col0, nh, gcol, blk, dst, dst_buf):
        w = nh * 64
        src = psum[p][:, col0:col0 + w]
        t_sq, t_n = tmr.next(), tmr.next()
        s_ss = smr.next()
        S.op("act", "activation", reads=[b_ps[p]], writes=[b_tm[t_sq]], out=tm[t_sq][:, 0:w], in_=src, func=AF.Square)
        S.op("dve", "tensor_reduce", reads=[b_tm[t_sq]], writes=[b_sm[s_ss]], out=sm[s_ss][:, 0:nh],
             in_=tm[t_sq][:, 0:w].rearrange("p (h d) -> p h d", h=nh), axis=AX.X, op=ALU.add)
        S.op("act", "activation", reads=[b_sm[s_ss], b_eps], writes=[b_sm[s_ss]], out=sm[s_ss][:, 0:nh],
             in_=sm[s_ss][:, 0:nh], func=AF.Ln, bias=epst[:, 0:1], scale=1.0 / 64)
        S.op("act", "activation", reads=[b_sm[s_ss]], writes=[b_sm[s_ss]], out=sm[s_ss][:, 0:nh], in_=sm[s_ss][:, 0:nh],
             func=AF.Exp, scale=-0.5)
        n3 = tm[t_n][:, 0:w].rearrange("p (h d) -> p h d", h=nh)
        S.op("dve", "tensor_tensor", reads=[b_ps[p], b_sm[s_ss]], writes=[b_tm[t_n]], out=n3,
             in0=src.rearrange("p (h d) -> p h d", h=nh), in1=sm[s_ss][:, 0:nh].unsqueeze(2).to_broadcast([128, nh, 64]),
             op=ALU.mult)
        S.op("dve", "tensor_tensor", reads=[b_tm[t_n], b_bcl], writes=[b_tm[t_n]], out=n3, in0=n3,
             in1=bcl[:, gcol:gcol + 64].unsqueeze(1).to_broadcast([128, nh, 64]), op=ALU.mult)
        n4 = tm[t_n][:, 0:w].rearrange("p (h t d) -> p h t d", h=nh, t=2)
        x1, x2 = n4[:, :, 0, :], n4[:, :, 1, :]
        cosb = cs_sb[:, blk, 0:32].unsqueeze(1).to_broadcast([128, nh, 32])
        sinb = cs_sb[:, blk, 32:64].unsqueeze(1).to_broadcast([128, nh, 32])
        ta = tm[t_sq][:, 0:w].rearrange("p (h t d) -> p h t d", h=nh, t=2)
        d4 = dst.rearrange("p (h t d) -> p h t d", h=nh, t=2)
        rb = [b_tm[t_n], b_cs]
        S.op("dve", "tensor_tensor", reads=rb, writes=[b_tm[t_sq]], out=ta[:, :, 0, :], in0=x1, in1=cosb, op=ALU.mult)
        S.op("dve", "tensor_tensor", reads=rb, writes=[b_tm[t_sq]], out=ta[:, :, 1, :], in0=x2, in1=sinb, op=ALU.mult)
        S.op("dve", "tensor_tensor", reads=[b_tm[t_sq]], writes=[dst_buf], out=d4[:, :, 0, :], in0=ta[:, :, 0, :],
             in1=ta[:, :, 1, :], op=ALU.subtract)
        t_p = tmr.next()
        tb = tm[t_p][:, 0:w].rearrange("p (h t d) -> p h t d", h=nh, t=2)
        S.op("pool", "tensor_tensor", reads=rb, writes=[b_tm[t_p]], out=tb[:, :, 0, :], in0=x2, in1=cosb, op=ALU.mult)
        S.op("pool", "tensor_tensor", reads=rb, writes=[b_tm[t_p]], out=tb[:, :, 1, :], in0=x1, in1=sinb, op=ALU.mult)
        S.op("pool", "tensor_tensor", reads=[b_tm[t_p]], writes=[dst_buf], out=d4[:, :, 1, :], in0=tb[:, :, 0, :],
             in1=tb[:, :, 1, :], op=ALU.add)

    def gelu_tanh(eng_pool, src, src_bufs, dst, dst_bufs, n):
        S.op("act", "activation", reads=src_bufs, writes=dst_bufs, out=dst, in_=src, func=AF.Gelu_apprx_tanh)

    S.dma("sp", None, cf[:], cf_in[:, :], writes=[b_cf])
    S.dma("pool", None, cmb[:], cm_in[:, :], writes=[b_cmb])
    S.dma("sp", None, nrm_sb[:], nrm_in[:, :], writes=[b_nrm])
    S.dma("sp", None, cs_sb[:], cs_in.rearrange("(b p) d -> p b d", p=128), writes=[b_cs])
    S.op("dve", "memset", writes=[b_ones], ap=ones_bf[:], constant=1.0)
    S.op("dve", "memset", writes=[b_eps], ap=epst[:], constant=EPS)
    S.op("dve", "memset", writes=b_Vx, ap=Vx[:], constant=1.0)
    S.op("dve", "memset", writes=b_kT, ap=kT[64:128], constant=0.0)
    for i_ in range(2):
        S.op("dve", "memset", writes=[b_qT[i_]], ap=qTb[i_][64:128], constant=0.0)
    S.op("dve", "tensor_copy", reads=[b_cf], writes=[b_identbf], out=ident_bf[:], in_=ident)
    p = psr.next()
    S.op("pe", "transpose", reads=[b_nrm, b_cf], writes=[b_ps[p]], out=psum[p][:, 0:64], in_=nrm_sb[:], identity=cf[0:64, 0:64])
    S.op("dve", "tensor_copy", reads=[b_ps[p]], writes=[b_gT], out=gT[:], in_=psum[p][:, 0:64])

    for l in range(2):
        plan_layer(l)
    plan_one = len(W.plan) // 2
    W.plan = W.plan + W.plan
    W.start()

    def run_pipeline(gens):
        active = []
        it = iter(gens)
        while True:
            nxt = next(it, None)
            keep = []
            for g in reversed(active):
                try:
                    next(g)
                    keep.append(g)
                except StopIteration:
                    pass
            keep.reverse()
            if nxt is not None:
                try:
                    next(nxt)
                    keep.append(nxt)
                except StopIteration:
                    pass
            active = keep
            if nxt is None and not active:
                break

    def load_x_blocks(blocks, colof):
        for b in blocks:
            for hf in range(2):
                xi = xir.next()
                S.dma("sp", None, xin[xi][:], x_in[b * 128:(b + 1) * 128, hf * 1024:(hf + 1) * 1024], writes=[b_xin[xi]])
                for g in range(2):
                    p = psr.next()
                    for j in range(4):
                        cc = g * 4 + j
                        S.op("pe", "transpose", reads=[b_xin[xi], b_cf], writes=[b_ps[p]], inc=(j == 3),
                             out=psum[p][:, j * 128:(j + 1) * 128], in_=xin[xi][:, cc * 128:(cc + 1) * 128], identity=ident)
                    c0 = hf * 8 + g * 4
                    if b == 0:
                        evac_copy(evr.next(), h0T[:, c0:c0 + 4, :], psum[p][:].rearrange("p (j n) -> p j n", j=4),
                                  [b_ps[p]], [b_h0T])
                    else:
                        k = colof[b] // 128
                        evac_copy(evr.next(), xT[:, c0:c0 + 4, colof[b]:colof[b] + 128],
                                  psum[p][:].rearrange("p (j n) -> p j n", j=4), [b_ps[p]],
                                  [b_xT[(c0 + j, k)] for j in range(4)])

    def layer(wave, l):
        first = (wave == 0)
        if first and l == 0:
            F = [1, 2, 3, 4, 5]
            colof = {b: (b - 1) * 128 for b in F}
            kvb = 0
            slot = {0: 0, 1: 1, 2: 2, 3: 3, 4: 4, 5: 6}
            tt = [(0, 320), (320, 320)]
            ntiles = tt
        elif first:
            F = [2, 3, 4, 5]
            colof = {b: (b - 1) * 128 for b in [1, 2, 3, 4, 5]}
            kvb = 1
            slot = {1: 0, 2: 1, 3: 2, 4: 3, 5: 5}
            tt = [(128, 512)]
            ntiles = [(0, 320), (320, 320)]
        else:
            F = [6, 7, 8, 9]
            colof = {b: (b - 6) * 128 for b in F}
            kvb = None
            slot = {6: 0, 7: 1, 8: 2, 9: 3, 5: (6 if l == 0 else 5)}
            tt = [(0, 512)]
            ntiles = tt
        gmix, gffn = l * 32, l * 32 + 16

        S.dma("sp", None, bcl[:], bc_in[l:l + 1, :].partition_broadcast(128), writes=[b_bcl])
        S.op("act", "activation", reads=[b_bcl], writes=[b_esink], out=esink[:], in_=bcl[:, 128:144], func=AF.Exp)
        for hh in range(2):
            tf, tt_ = tmr.next(), tmr.next()
            bf_, bt_ = tm[tf][0:1, 0:512], tm[tt_][0:1, 0:512]
            hsl = slice(hh * 512, (hh + 1) * 512)
            S.dma("sp", None, bf_, bsp_in[l:l + 1, hsl], writes=[b_tm[tf]])
            S.op("dve", "tensor_copy", reads=[b_tm[tf]], writes=[b_bsp], out=bsp_hi[0:1, hsl], in_=bf_)
            S.op("dve", "tensor_copy", reads=[b_bsp], writes=[b_tm[tt_]], out=bt_, in_=bsp_hi[0:1, hsl])
            S.op("dve", "tensor_tensor", reads=[b_tm[tf], b_tm[tt_]], writes=[b_tm[tt_]], out=bt_, in0=bf_, in1=bt_, op=ALU.subtract)
            S.op("dve", "tensor_copy", reads=[b_tm[tt_]], writes=[b_bsp], out=bsp_lo[0:1, hsl], in_=bt_)
        for hf in range(2):
            xi = xir.next()
            S.dma("sp", None, xin[xi][:, 0:512].rearrange("p (g j) -> p g j", g=4),
                  wsp_in[l, hf * 4:(hf + 1) * 4].rearrange("g i j -> i g j"), writes=[b_xin[xi]])
            p = psr.next()
            for g in range(4):
                S.op("pe", "transpose", reads=[b_xin[xi], b_cf], writes=[b_ps[p]], inc=(g == 3),
                     out=psum[p][:, g * 128:(g + 1) * 128], in_=xin[xi][:, g * 128:(g + 1) * 128], identity=ident)
            S.op("dve", "tensor_tensor", reads=[b_ps[p], b_cf], writes=[b_WsT],
                 out=WsT[:, hf * 4:(hf + 1) * 4, :], in0=psum[p][:].rearrange("p (g i) -> p g i", g=4),
                 in1=tri.unsqueeze(1).to_broadcast([128, 4, 128]), op=ALU.mult)

        if l == 0:
            load_x_blocks(([0] if first else []) + F, colof)

        if kvb == 0:
            rms_stats(lambda c: h0T[:, c, :], 128, lambda c: [b_h0T])
            for c in range(NCH):
                S.op("dve", "scalar_tensor_tensor", reads=[b_h0T, b_rstd, b_gT], writes=[b_h0T],
                     out=h0T[:, c, :], in0=h0T[:, c, :], scalar=gT[:, gmix + c:gmix + c + 1], in1=rstd[:, 0:128],
                     op0=ALU.mult, op1=ALU.mult)
        rmsnorm_cols(gmix, ntiles)
        dump(f"hT_{wave}_{l}", cb[:, HT:HT + 16, :], [b_cb[(HT + c, k)] for c in range(16) for k in range(5)], [128, 16, TW], BF16)
        if stop_after == f"norm_{wave}_{l}":
            return False

        def hsrc(b, c):
            if b == 0:
                return h0T[:, c, :], [b_h0T]
            return cbv(HT + c, colof[b], 128), [b_cb[(HT + c, colof[b] // 128)]]

        def tm_proj(b, slots, p):
            for si, sl in enumerate(slots):
                wvv = wv(sl, 0, 16, 256)
                for c in range(NCH):
                    ha, hb = hsrc(b, c)
                    S.op("pe", "matmul", reads=hb + wb(sl), writes=[b_ps[p]], inc=(c == NCH - 1 and si == 1),
                         out=psum[p][:, si * 256:(si + 1) * 256], lhsT=ha, rhs=wvv[:, c, :], start=(c == 0), stop=(c == NCH - 1))

        sl_k, sl_v = W.get(), W.get()
        kvblocks = ([kvb] if kvb is not None else []) + F

        def kv_item(b, last):
            p = psA.next()
            tm_proj(b, [sl_k, sl_v], p)
            if last:
                W.done(2)
            yield
            s_ = slot[b]
            S.op("act", "activation", reads=[b_ps[p]], writes=[b_Vx[s_]], out=Vx[:, s_, :, 0:64],
                 in_=psum[p][:, 256:512].rearrange("p (h d) -> p h d", h=4), func=AF.Copy)
            qi = qrr.next()
            head_norm_rope(p, 0, 4, 64, b, qrot[qi][:, 0:256], b_qrot[qi])
            yield
            pt = psB.next()
            ptb = psum[pt][:].bitcast(BF16)
            for h in range(4):
                S.op("pe", "transpose", reads=[b_qrot[qi], b_identbf], writes=[b_ps[pt]], inc=(h == 3),
                     out=ptb[0:64, h * 128:(h + 1) * 128], in_=qrot[qi][:, h * 64:(h + 1) * 64], identity=ident_bf[:])
            evac_copy("act", kT[0:64, s_, :, :], ptb[0:64, 0:512].rearrange("p (h n) -> p h n", h=4), [b_ps[pt]], [b_kT[s_]])

        qslabs = {}

        def attn_item(j, b, firstj, lastj):
            if firstj:
                qslabs[j] = (W.get(), W.get())
            sl_a, sl_b = qslabs[j]
            p = psA.next()
            tm_proj(b, [sl_a, sl_b], p)
            if lastj:
                W.done(2)
                if j == 1:
                    ust["q1done"] = True
                    W.done(ust["pend"])
                    ust["pend"] = 0
            yield
            qi = qrr.next()
            head_norm_rope(p, 0, 8, 0, b, qrot[qi][:, 0:512], b_qrot[qi])
            yield
            pt = psB.next()
            ptb = psum[pt][:].bitcast(BF16)
            for h in range(8):
                S.op("pe", "transpose", reads=[b_qrot[qi], b_identbf], writes=[b_ps[pt]], inc=(h == 7),
                     out=ptb[0:64, h * 128:(h + 1) * 128], in_=qrot[qi][:, h * 64:(h + 1) * 64], identity=ident_bf[:])
            qt = qtr.next()
            evac_copy("act", qTb[qt][0:64, :, :], ptb[0:64, :].rearrange("p (h n) -> p h n", h=8), [b_ps[pt]], [b_qT[qt]])
            yield
            sc, sp_ = slot[b], slot[b - 1]
            pis_all = []
            for kk in range(2):
                kvh = 2 * j + kk
                rhs_q = qTb[qt][:, kk * 4:(kk + 1) * 4, :].rearrange("p h n -> p (h n)")
                pis = []
                for (ss, mk) in ((sp_, (maskfirst if (first and b == 2) else maskprev)), (sc, maskcur)):
                    ps_ = psB.next()
                    S.op("pe", "matmul", reads=[b_kT[ss], b_qT[qt]], writes=[b_ps[ps_]], inc=False,
                         out=psum[ps_][:], lhsT=kT[:, ss, kvh, :], rhs=rhs_q, start=True, stop=False)
                    S.op("pe", "matmul", reads=[b_identbf, b_cmb], writes=[b_ps[ps_]], inc=True,
                         out=psum[ps_][:], lhsT=ident_bf[:], rhs=mk, start=False, stop=True)
                    pi = Pr.next()
                    S.op("act", "activation", reads=[b_ps[ps_]], writes=[b_P[pi]], out=Pb[pi][:], in_=psum[ps_][:],
                         func=AF.Exp, scale=0.125)
                    pis.append((pi, ss))
                pis_all.append(pis)
            yield
            ai = atr.next()
            for kk in range(2):
                kvh = 2 * j + kk
                pis = pis_all[kk]
                po = psB.next()
                pov = psum[po][:].rearrange("p (g n) -> p g n", g=4)
                for g in range(4):
                    for idx, (pi, ss) in enumerate(pis):
                        S.op("pe", "matmul", reads=[b_P[pi], b_Vx[ss]], writes=[b_ps[po]], inc=(g == 3 and idx == 1),
                             out=pov[:, g, 0:65], lhsT=Pb[pi][:, g * 128:(g + 1) * 128], rhs=Vx[:, ss, kvh, :],
                             start=(idx == 0), stop=(idx == 1))
                sd = smr.next()
                hq0 = j * 8 + kk * 4
                S.op("dve", "tensor_tensor", reads=[b_ps[po], b_esink], writes=[b_sm[sd]], out=sm[sd][:, 0:4].unsqueeze(2),
                     in0=pov[:, :, 64:65], in1=esink[:, hq0:hq0 + 4].unsqueeze(2), op=ALU.add)
                S.op("dve", "reciprocal", reads=[b_sm[sd]], writes=[b_sm[sd]], out=sm[sd][:, 0:4], in_=sm[sd][:, 0:4])
                S.op("dve", "tensor_tensor", reads=[b_ps[po], b_sm[sd]], writes=[b_atm[ai]],
                     out=atm[ai][:, kk * 256:(kk + 1) * 256].rearrange("p (g d) -> p g d", g=4), in0=pov[:, :, 0:64],
                     in1=sm[sd][:, 0:4].unsqueeze(2).to_broadcast([128, 4, 64]), op=ALU.mult)
            yield
            pt = psB.next()
            ptb = psum[pt][:].bitcast(BF16)
            for cc in range(4):
                S.op("pe", "transpose", reads=[b_atm[ai], b_identbf], writes=[b_ps[pt]], inc=(cc == 3),
                     out=ptb[:, cc * 128:(cc + 1) * 128], in_=atm[ai][:, cc * 128:(cc + 1) * 128], identity=ident_bf[:])
            k = colof[b] // 128
            evac_copy("act", cb[:, AT + j * 4:AT + j * 4 + 4, colof[b]:colof[b] + 128],
                      ptb[:, 0:512].rearrange("p (c n) -> p c n", c=4), [b_ps[pt]], [b_cb[(AT + j * 4 + cc, k)] for cc in range(4)])

        ust = {"q1done": False, "pend": 0}
        uslabs = {}

        def u_item(j, oc, ti, firstj, lastj):
            if firstj:
                uslabs[j] = W.get()
            sl = uslabs[j]
            wvv = wv(sl, 0, 16, 256)
            g = j * 2 + oc
            off, n = tt[ti]
            p = psB.next()
            for c in range(NCH):
                S.op("pe", "matmul", reads=cbb(HT + c, off, n) + wb(sl), writes=[b_ps[p]], inc=(c == NCH - 1),
                     out=psum[p][:, 0:n], lhsT=wvv[:, c, oc * 128:(oc + 1) * 128], rhs=cbv(HT + c, off, n),
                     start=(c == 0), stop=(c == NCH - 1))
            gelu_tanh(None, psum[p][:, 0:n], [b_ps[p]], cbv(GT + g, off, n), cbb(GT + g, off, n), n)
            if lastj:
                if ust["q1done"]:
                    W.done(1)
                else:
                    ust["pend"] += 1
            return
            yield

        def u_items(js):
            res = []
            for j in js:
                grp = [(oc, ti) for oc in range(2) for ti in range(len(tt))]
                for idx, (oc, ti) in enumerate(grp):
                    res.append(u_item(j, oc, ti, idx == 0, idx == len(grp) - 1))
            return res

        items = [kv_item(b, b == kvblocks[-1]) for b in kvblocks]
        items += [attn_item(0, b, b == F[0], b == F[-1]) for b in F]
        a1 = [attn_item(1, b, b == F[0], b == F[-1]) for b in F]
        u01, u23 = u_items([0, 1]), u_items([2, 3])
        mixed = [a1[0]]
        ui = 0
        for it_ in a1[1:]:
            per = -(-len(u01) // (len(a1) - 1))
            mixed += u01[ui:ui + per]
            ui += per
            mixed.append(it_)
        mixed += u01[ui:]
        items += mixed + u23
        run_pipeline(items)
        dump(f"kT_{wave}_{l}", kT[0:64], b_kT, [64, 7, 4, 128], BF16)
        dump(f"Vx_{wave}_{l}", Vx[:], b_Vx, [128, 7, 4, 65], BF16)
        dump(f"attnT_{wave}_{l}", cb[:, AT:AT + 8, :], [b_cb[(AT + c, k)] for c in range(8) for k in range(5)], [128, 8, TW], BF16)
        if stop_after == f"attn_{wave}_{l}":
            return False

        vslabs = {}

        def sgu_item(j, b, firstj, lastj):
            if firstj:
                vslabs[j] = (W.get(), W.get())
                S.dma("sp", None, lngb[:], lngb_in[l, :, j * 512:(j + 1) * 512].partition_broadcast(128), writes=[b_lngb])
            sl_a, sl_b = vslabs[j]
            if True:
                p = psA.next()
                tm_proj(b, [sl_a, sl_b], p)
                if lastj:
                    W.done(2)
                yield
                tg = tmr.next()
                vg = tm[tg][:, 0:512]
                gelu_tanh(None, psum[p][:, 0:512], [b_ps[p]], vg, [b_tm[tg]], 512)
                t2 = tmr.next()
                s1, s2 = smr.next(), smr.next()
                vg3 = vg.rearrange("p (g d) -> p g d", g=4)
                S.op("dve", "tensor_reduce", reads=[b_tm[tg]], writes=[b_sm[s1]], out=sm[s1][:, 0:4], in_=vg3, axis=AX.X, op=ALU.add)
                S.op("act", "activation", reads=[b_tm[tg]], writes=[b_tm[t2]], out=tm[t2][:, 0:512], in_=vg, func=AF.Square)
                S.op("dve", "tensor_reduce", reads=[b_tm[t2]], writes=[b_sm[s2]], out=sm[s2][:, 0:4],
                     in_=tm[t2][:, 0:512].rearrange("p (g d) -> p g d", g=4), axis=AX.X, op=ALU.add)
                S.op("dve", "tensor_scalar", reads=[b_sm[s1]], writes=[b_sm[s1]], out=sm[s1][:, 0:4], in0=sm[s1][:, 0:4],
                     scalar1=1.0 / 128, scalar2=None, op0=ALU.mult)
                S.op("dve", "tensor_tensor", reads=[b_sm[s1]], writes=[b_sm[s1]], out=sm[s1][:, 4:8], in0=sm[s1][:, 0:4],
                     in1=sm[s1][:, 0:4], op=ALU.mult)
                S.op("dve", "scalar_tensor_tensor", reads=[b_sm[s2], b_sm[s1]], writes=[b_sm[s2]], out=sm[s2][:, 0:4],
                     in0=sm[s2][:, 0:4], scalar=1.0 / 128, in1=sm[s1][:, 4:8], op0=ALU.mult, op1=ALU.subtract)
                S.op("act", "activation", reads=[b_sm[s2], b_eps], writes=[b_sm[s2]], out=sm[s2][:, 0:4], in_=sm[s2][:, 0:4],
                     func=AF.Ln, bias=epst[:, 0:1], scale=1.0)
                S.op("act", "activation", reads=[b_sm[s2]], writes=[b_sm[s2]], out=sm[s2][:, 0:4], in_=sm[s2][:, 0:4],
                     func=AF.Exp, scale=-0.5)
                S.op("dve", "tensor_tensor", reads=[b_tm[tg], b_sm[s1]], writes=[b_tm[tg]], out=vg3, in0=vg3,
                     in1=sm[s1][:, 0:4].unsqueeze(2).to_broadcast([128, 4, 128]), op=ALU.subtract)
                S.op("dve", "tensor_tensor", reads=[b_tm[tg], b_sm[s2]], writes=[b_tm[tg]], out=vg3, in0=vg3,
                     in1=sm[s2][:, 0:4].unsqueeze(2).to_broadcast([128, 4, 128]), op=ALU.mult)
                S.op("dve", "tensor_tensor", reads=[b_tm[tg], b_lngb], writes=[b_tm[tg]], out=vg, in0=vg, in1=lngb[:, 0, :], op=ALU.mult)
                vi = vnr.next()
                S.op("dve", "tensor_tensor", reads=[b_tm[tg], b_lngb], writes=[b_vn[vi]], out=vnb[vi][:], in0=vg, in1=lngb[:, 1, :], op=ALU.add)
                yield
                pspt = psB.next()
                for g4 in range(4):
                    g = j * 4 + g4
                    o_ = psum[pspt][:, g4 * 128:(g4 + 1) * 128]
                    S.op("pe", "matmul", reads=[b_vn[vi], b_WsT], writes=[b_ps[pspt]], inc=False, out=o_,
                         lhsT=vnb[vi][:, g4 * 128:(g4 + 1) * 128], rhs=WsT[:, g, :], start=True, stop=False)
                    S.op("pe", "matmul", reads=[b_ones, b_bsp], writes=[b_ps[pspt]], inc=False, out=o_,
                         lhsT=ones_bf[0:1, :], rhs=bsp_hi[0:1, g * 128:(g + 1) * 128], start=False, stop=False)
                    S.op("pe", "matmul", reads=[b_ones, b_bsp], writes=[b_ps[pspt]], inc=(g4 == 3), out=o_,
                         lhsT=ones_bf[0:1, :], rhs=bsp_lo[0:1, g * 128:(g + 1) * 128], start=False, stop=True)
                k = colof[b] // 128
                gsl = cb[:, GT + j * 4:GT + j * 4 + 4, colof[b]:colof[b] + 128]
                if "sgudbg_u" in dumps:
                    S.op("dve", "tensor_copy", reads=[b_ps[pspt]], writes=[b_sm[0]], out=sm[0][:, 0:1], in_=psum[pspt][:, 0:1])
                elif "sgudbg_s" in dumps:
                    S.op("dve", "tensor_copy", reads=[b_ps[pspt]], writes=[b_cb[(GT + j * 4 + g4, k)] for g4 in range(4)], out=gsl,
                         in_=psum[pspt][:].rearrange("p (g i) -> p g i", g=4))
                else:
                    S.op("dve", "tensor_tensor", reads=[b_ps[pspt]] + [b_cb[(GT + j * 4 + g4, k)] for g4 in range(4)],
                         writes=[b_cb[(GT + j * 4 + g4, k)] for g4 in range(4)], out=gsl,
                         in0=psum[pspt][:].rearrange("p (g i) -> p g i", g=4), in1=gsl, op=ALU.mult)

        run_pipeline([sgu_item(j, b, b == F[0], b == F[-1]) for j in range(2) for b in F])
        dump(f"gatedT_{wave}_{l}", cb[:, GT:GT + 8, :], [b_cb[(GT + c, k)] for c in range(8) for k in range(5)], [128, 8, TW], BF16)
        if stop_after == f"sgu_{wave}_{l}":
            return False

        def fm_group(p, n, off, pieces):
            last = len(pieces) - 1
            for i, (lt, rh, rb) in enumerate(pieces):
                S.op("pe", "matmul", reads=rb, writes=[b_ps[p]], inc=(i == last), out=psum[p][:, 0:n], lhsT=lt, rhs=rh,
                     start=(i == 0), stop=(i == last))

        for s in range(8):
            s2, s1 = W.get(), W.get()
            wa, wsg = wv(s1, 0, 8, 256), wv(s1, 2048, 8, 256)
            wga = wv(s2, 0, 16, 256)
            for oc in range(2):
                o = s * 2 + oc
                osl = slice(oc * 128, (oc + 1) * 128)
                for (off, n) in tt:
                    pa, pga = psr.next(), psr.next()
                    fm_group(pa, n, off, [(wa[:, c, osl], cbv(AT + c, off, n), cbb(AT + c, off, n) + wb(s1)) for c in range(8)])
                    fm_group(pga, n, off, [(wga[:, c, osl], cbv(HT + c, off, n), cbb(HT + c, off, n) + wb(s2)) for c in range(16)])
                    ta = tmr.next()
                    S.op("act", "activation", reads=[b_ps[pga]], writes=[b_tm[ta]], out=tm[ta][:, 0:n], in_=psum[pga][:, 0:n], func=AF.Sigmoid)
                    S.op("dve", "tensor_tensor", reads=[b_ps[pa], b_tm[ta]], writes=cbb(MG + o, off, n), out=cbv(MG + o, off, n),
                         in0=psum[pa][:, 0:n], in1=tm[ta][:, 0:n], op=ALU.mult)
            W.done(1)
            s3 = W.get()
            wgb = wv(s3, 0, 16, 256)
            for oc in range(2):
                o = s * 2 + oc
                osl = slice(oc * 128, (oc + 1) * 128)
                for (off, n) in tt:
                    pb, pgb = psr.next(), psr.next()
                    fm_group(pb, n, off, [(wsg[:, c, osl], cbv(GT + c, off, n), cbb(GT + c, off, n) + wb(s1)) for c in range(8)])
                    fm_group(pgb, n, off, [(wgb[:, c, osl], cbv(HT + c, off, n), cbb(HT + c, off, n) + wb(s3)) for c in range(16)])
                    tb = tmr.next()
                    S.op("act", "activation", reads=[b_ps[pgb]], writes=[b_tm[tb]], out=tm[tb][:, 0:n], in_=psum[pgb][:, 0:n], func=AF.Sigmoid)
                    S.op("dve", "tensor_tensor", reads=[b_ps[pb], b_tm[tb]], writes=[b_tm[tb]], out=tm[tb][:, 0:n], in0=psum[pb][:, 0:n],
                         in1=tm[tb][:, 0:n], op=ALU.mult)
                    S.op("dve", "tensor_tensor", reads=[b_tm[tb]] + cbb(MG + o, off, n), writes=cbb(MG + o, off, n), out=cbv(MG + o, off, n),
                         in0=tm[tb][:, 0:n], in1=cbv(MG + o, off, n), op=ALU.add)
            W.done(2)
        dump(f"mergedT_{wave}_{l}", cb[:, MG:MG + 16, :], [b_cb[(MG + c, k)] for c in range(16) for k in range(5)], [128, 16, TW], BF16)

        for s in range(8):
            sl = W.get()
            wvv = wv(sl, 0, 16, 256)
            for oc in range(2):
                o = s * 2 + oc
                for (off, n) in tt:
                    p = psr.next()
                    fm_group(p, n, off, [(wvv[:, c, oc * 128:(oc + 1) * 128], cbv(MG + c, off, n), cbb(MG + c, off, n) + wb(sl))
                                         for c in range(16)])
                    S.op("dve", "tensor_tensor", reads=[b_ps[p]] + xTb(o, off, n), writes=xTb(o, off, n), out=xT[:, o, off:off + n],
                         in0=psum[p][:, 0:n], in1=xT[:, o, off:off + n], op=ALU.add)
            W.done(1)
        dump(f"xmix_{wave}_{l}", xT[:], list(b_xT.values()), [128, 16, TW], F32)
        if stop_after == f"mix_{wave}_{l}":
            return False

        rmsnorm_cols(gffn, tt)
        for (fc0, nfc) in FFN_PARTS:
            for jj in range(nfc // 2):
                sg_, su_ = W.get(), W.get()
                wg, wu = wv(sg_, 0, 16, 256), wv(su_, 0, 16, 256)
                for oc in range(2):
                    a_i = ACT0 + jj * 2 + oc
                    osl = slice(oc * 128, (oc + 1) * 128)
                    for (off, n) in tt:
                        pg, pu = psr.next(), psr.next()
                        fm_group(pg, n, off, [(wg[:, c, osl], cbv(HT + c, off, n), cbb(HT + c, off, n) + wb(sg_)) for c in range(16)])
                        fm_group(pu, n, off, [(wu[:, c, osl], cbv(HT + c, off, n), cbb(HT + c, off, n) + wb(su_)) for c in range(16)])
                        ts = tmr.next()
                        S.op("act", "activation", reads=[b_ps[pg]], writes=[b_tm[ts]], out=tm[ts][:, 0:n], in_=psum[pg][:, 0:n], func=AF.Silu)
                        S.op("dve", "tensor_tensor", reads=[b_ps[pu], b_tm[ts]], writes=cbb(a_i, off, n), out=cbv(a_i, off, n),
                             in0=psum[pu][:, 0:n], in1=tm[ts][:, 0:n], op=ALU.mult)
                W.done(2)
            for s in range(8):
                sl = W.get()
                wvv = wv(sl, 0, nfc, 256)
                for oc in range(2):
                    o = s * 2 + oc
                    for (off, n) in tt:
                        p = psr.next()
                        fm_group(p, n, off, [(wvv[:, c, oc * 128:(oc + 1) * 128], cbv(ACT0 + c, off, n), cbb(ACT0 + c, off, n) + wb(sl))
                                             for c in range(nfc)])
                        S.op("dve", "tensor_tensor", reads=[b_ps[p]] + xTb(o, off, n), writes=xTb(o, off, n), out=xT[:, o, off:off + n],
                             in0=psum[p][:, 0:n], in1=xT[:, o, off:off + n], op=ALU.add)
                W.done(1)
        dump(f"xout_{wave}_{l}", xT[:], list(b_xT.values()), [128, 16, TW], F32)
        if stop_after == f"layer_{wave}_{l}":
            return False

        if l == 1:
            for b in F:
                for hf in range(2):
                    xi = xir.next()
                    for g in range(2):
                        p = psr.next()
                        for jx in range(4):
                            c = hf * 8 + g * 4 + jx
                            S.op("pe", "transpose", reads=[b_xT[(c, colof[b] // 128)], b_cf], writes=[b_ps[p]], inc=(jx == 3),
                                 out=psum[p][:, jx * 128:(jx + 1) * 128], in_=xT[:, c, colof[b]:colof[b] + 128], identity=ident)
                        evac_copy(evr.next(), xin[xi][:, g * 512:(g + 1) * 512], psum[p][:], [b_ps[p]], [b_xin[xi]])
                    out_events.append(S.dma("sp", None, out_d[(b - 2) * 128:(b - 1) * 128, hf * 1024:(hf + 1) * 1024], xin[xi][:],
                                            reads=[b_xin[xi]]))
        return True

    ok = True
    for wave in range(2):
        for l in range(2):
            if ok:
                ok = layer(wave, l)
    for ev in out_events:
        S.wait_event("sp", ev)
    S.emit()
    return nc, S


def _consts():
    ident = np.eye(128, dtype=np.float32)
    j = np.arange(128)[:, None]
    i = np.arange(128)[None, :]
    tri = (j <= i).astype(np.float32)
    cur = np.where(j <= i, 0.0, NEG).astype(np.float32)
    prev = np.where(j > i, 0.0, NEG).astype(np.float32)
    return ident, tri, np.tile(cur, (1, 4)), np.tile(prev, (1, 4))


def _rope_tables():
    pos = np.arange(2048, dtype=np.float32)
    inv = np.power(np.float32(10000.0), -np.arange(0, 64, 2, dtype=np.float32) / np.float32(64)).astype(np.float32)
    ang = (pos[:, None] * inv[None, :]).astype(np.float32)
    return np.cos(ang).astype(np.float32), np.sin(ang).astype(np.float32)


def make_in_maps(inputs):
    x = np.ascontiguousarray(inputs["x"], dtype=np.float32)
    ident, tri, cur, prev = _consts()
    cos, sin = _rope_tables()
    cf = np.concatenate([ident, tri], axis=1)
    nrm = np.zeros((64, 128), np.float32)
    for l in range(2):
        nrm[l * 32:l * 32 + 16] = inputs["mix_norm"][l].reshape(16, 128)
        nrm[l * 32 + 16:l * 32 + 32] = inputs["ffn_norm"][l].reshape(16, 128)
    bc = np.concatenate([inputs["q_norm"], inputs["k_norm"], inputs["sinks"]], axis=1).astype(np.float32)
    lngb = np.stack([inputs["sgu_ln_g"], inputs["sgu_ln_b"]], axis=1).astype(np.float32)
    bsp = inputs["b_spatial"].reshape(2, 1024).astype(np.float32)
    shared = {
        "cf": cf, "nrm": nrm, "bc": bc, "lngb": lngb, "bsp": bsp,
        "w_spatial": np.ascontiguousarray(inputs["w_spatial"], dtype=np.float32),
        "w_in": np.ascontiguousarray(inputs["w_in"], dtype=np.float32),
        "w_attn": np.ascontiguousarray(inputs["w_attn_branch"], dtype=np.float32),
        "w_sgu": np.ascontiguousarray(inputs["w_sgu_branch"], dtype=np.float32),
        "w_out": np.ascontiguousarray(inputs["w_out"], dtype=np.float32),
        "w_gate": np.ascontiguousarray(inputs["w_gate"], dtype=np.float32),
        "w_up": np.ascontiguousarray(inputs["w_up"], dtype=np.float32),
        "w_down": np.ascontiguousarray(inputs["w_down"], dtype=np.float32),
    }
    maps = []
    for core in range(8):
        b, h = core // 2, core % 2
        start = h * 1024
        xc = np.zeros((1280, D), np.float32)
        csc = np.zeros((1280, 64), np.float32)
        lo = start - 256
        src0 = max(lo, 0)
        xc[src0 - lo:] = x[b, src0:start + 1024]
        csc[src0 - lo:, 0:32] = cos[src0:start + 1024]
        csc[src0 - lo:, 32:64] = sin[src0:start + 1024]
        first = np.full_like(prev, NEG) if h == 0 else prev
        cm = np.concatenate([cur, prev, first], axis=1)
        m = dict(shared)
        m.update({"x": xc, "cs": csc, "cm": cm})
        maps.append(m)
    return maps


_CACHE = {}


def kernel(**inputs):
    inputs = {k: np.asarray(v) for k, v in inputs.items()}
    if "nc" not in _CACHE:
        _CACHE["nc"] = build()[0]
    nc = _CACHE["nc"]
    maps = make_in_maps(inputs)
    res = run_bass_kernel_spmd(nc, maps, core_ids=list(range(8)))
    out = np.zeros((4, 2048, D), np.float32)
    for core in range(8):
        b, h = core // 2, core % 2
        out[b, h * 1024:(h + 1) * 1024] = res.results[core]["out"]
    return out
```
